# Optimizing a Trainium2 kernel written in Bass

```python
import jax, jax.numpy as jnp
from jax import lax
import numpy as np

D_MODEL = 2048
BATCH = 4
SEQ = 4096
DEPTH = 1

CHUNK = 64
N_MEM = 256
EPS = 1e-6
D_MIX = D_MODEL
D_POOL = D_MIX // 2
POOL_WINDOWS = (2, 4, 8, 16)
N_POOL_GROUPS = len(POOL_WINDOWS)
POOL_GROUP_DIM = D_POOL // N_POOL_GROUPS
D_SGU = D_MIX - D_POOL
SGU_BLOCK = 128
N_SGU_HEADS = 8
SGU_HEAD_DIM = D_SGU // N_SGU_HEADS
D_IN = D_POOL + 2 * D_SGU
N_XATTN_HEADS = 4
XATTN_HEAD_DIM = D_MODEL // N_XATTN_HEADS
D_FF = ((8 * D_MODEL // 3 + 255) // 256) * 256

kernel_name = "hybrid_pool_sgu_memxattn_block"


def rmsnorm(x, g):
    x32 = x.astype(jnp.float32)
    y = x32 * lax.rsqrt(jnp.mean(x32 * x32, axis=-1, keepdims=True) + EPS)
    return (y * g.astype(jnp.float32)).astype(x.dtype)


def multiscale_pool(a, pool_w, pool_scale):
    B, S, _ = a.shape
    a32 = a.astype(jnp.float32)
    csum = jnp.cumsum(a32, axis=1)
    pos = jnp.arange(1, S + 1, dtype=jnp.float32)[None, :, None]
    outs = []
    for g, w in enumerate(POOL_WINDOWS):
        sl = slice(g * POOL_GROUP_DIM, (g + 1) * POOL_GROUP_DIM)
        c = csum[..., sl]
        prev = jnp.pad(c, ((0, 0), (w, 0), (0, 0)))[:, :S]
        mean = (c - prev) / jnp.minimum(pos, float(w))
        outs.append(mean - a32[..., sl])
    p = jnp.stack(outs, axis=2).astype(a.dtype)
    y = jnp.einsum('bsgc,gcd->bsgd', p, pool_w)
    return y.reshape(B, S, D_POOL) * pool_scale


def spatial_gating(uv, sgu_norm_g, w_spatial, b_spatial):
    B, S, _ = uv.shape
    u, v = uv[..., :D_SGU], uv[..., D_SGU:]
    v = rmsnorm(v, sgu_norm_g)
    v = v.reshape(B, S // SGU_BLOCK, SGU_BLOCK, N_SGU_HEADS, SGU_HEAD_DIM)
    t = jnp.arange(SGU_BLOCK)
    mask = (t[None, :] // CHUNK) <= (t[:, None] // CHUNK)
    ws = jnp.where(mask[None], w_spatial, 0.0)
    mixed = jnp.einsum('hts,bnshc->bnthc', ws, v)
    mixed = mixed + b_spatial.T[None, None, :, :, None]
    return u * mixed.reshape(B, S, D_SGU)


def memory_cross_attention(h, m, w_q, w_k, w_v, w_o):
    B, S, _ = h.shape
    M = m.shape[1]
    q = (h @ w_q).reshape(B, S, N_XATTN_HEADS, XATTN_HEAD_DIM)
    k = (m @ w_k).reshape(B, M, N_XATTN_HEADS, XATTN_HEAD_DIM)
    v = (m @ w_v).reshape(B, M, N_XATTN_HEADS, XATTN_HEAD_DIM)
    s = jnp.einsum('bshd,bmhd->bhsm', q, k).astype(jnp.float32) * (XATTN_HEAD_DIM ** -0.5)
    p = jax.nn.softmax(s, axis=-1).astype(v.dtype)
    o = jnp.einsum('bhsm,bmhd->bshd', p, v).reshape(B, S, D_MODEL)
    return o @ w_o


def swiglu(h, w_gate, w_up, w_down):
    return (jax.nn.silu(h @ w_gate) * (h @ w_up)) @ w_down


def setup_inputs(seed: int = 0) -> dict:
    key = jax.random.key(seed)
    ks = jax.random.split(key, 24)
    L = DEPTH
    f32 = jnp.float32

    def nrm(k, shape, scale):
        return jax.random.normal(k, shape, f32) * scale

    def gain(k, shape):
        return 1.0 + 0.02 * jax.random.normal(k, shape, f32)

    return {
        "x": jax.random.normal(ks[0], (BATCH, SEQ, D_MODEL), f32),
        "mem": jax.random.normal(ks[1], (BATCH, N_MEM, D_MODEL), f32),
        "norm_mix_g": gain(ks[2], (L, D_MODEL)),
        "w_in": nrm(ks[3], (L, D_MODEL, D_IN), D_MODEL ** -0.5),
        "pool_w": nrm(ks[4], (L, N_POOL_GROUPS, POOL_GROUP_DIM, POOL_GROUP_DIM), POOL_GROUP_DIM ** -0.5),
        "pool_scale": 1.0 + 0.1 * jax.random.normal(ks[5], (L, D_POOL), f32),
        "sgu_norm_g": gain(ks[6], (L, D_SGU)),
        "w_spatial": nrm(ks[7], (L, N_SGU_HEADS, SGU_BLOCK, SGU_BLOCK), SGU_BLOCK ** -0.5),
        "b_spatial": 1.0 + 0.1 * jax.random.normal(ks[8], (L, N_SGU_HEADS, SGU_BLOCK), f32),
        "w_out": nrm(ks[9], (L, D_MIX, D_MODEL), D_MIX ** -0.5),
        "norm_xattn_g": gain(ks[10], (L, D_MODEL)),
        "norm_mem_g": gain(ks[11], (L, D_MODEL)),
        "w_q": nrm(ks[12], (L, D_MODEL, D_MODEL), D_MODEL ** -0.5),
        "w_k": nrm(ks[13], (L, D_MODEL, D_MODEL), D_MODEL ** -0.5),
        "w_v": nrm(ks[14], (L, D_MODEL, D_MODEL), D_MODEL ** -0.5),
        "w_o": nrm(ks[15], (L, D_MODEL, D_MODEL), D_MODEL ** -0.5),
        "norm_ffn_g": gain(ks[16], (L, D_MODEL)),
        "w_gate": nrm(ks[17], (L, D_MODEL, D_FF), D_MODEL ** -0.5),
        "w_up": nrm(ks[18], (L, D_MODEL, D_FF), D_MODEL ** -0.5),
        "w_down": nrm(ks[19], (L, D_FF, D_MODEL), D_FF ** -0.5),
        "final_norm_g": gain(ks[20], (D_MODEL,)),
    }


def reference(x, mem, norm_mix_g, w_in, pool_w, pool_scale, sgu_norm_g, w_spatial,
              b_spatial, w_out, norm_xattn_g, norm_mem_g, w_q, w_k, w_v, w_o,
              norm_ffn_g, w_gate, w_up, w_down, final_norm_g):
    for l in range(DEPTH):
        h = rmsnorm(x, norm_mix_g[l])
        proj = h @ w_in[l]
        y_pool = multiscale_pool(proj[..., :D_POOL], pool_w[l], pool_scale[l])
        y_sgu = spatial_gating(proj[..., D_POOL:], sgu_norm_g[l], w_spatial[l], b_spatial[l])
        x = x + jnp.concatenate([y_pool, y_sgu], axis=-1) @ w_out[l]
        h = rmsnorm(x, norm_xattn_g[l])
        m = rmsnorm(mem, norm_mem_g[l])
        x = x + memory_cross_attention(h, m, w_q[l], w_k[l], w_v[l], w_o[l])
        h = rmsnorm(x, norm_ffn_g[l])
        x = x + swiglu(h, w_gate[l], w_up[l], w_down[l])
    return rmsnorm(x, final_norm_g)
```

```python
import numpy as np
from contextlib import ExitStack

import concourse.bass as bass
import concourse.mybir as mybir
from concourse.bass_utils import run_bass_kernel_spmd

F32 = mybir.dt.float32
BF16 = mybir.dt.bfloat16
AF = mybir.ActivationFunctionType
ALU = mybir.AluOpType

D = 2048
KC = 16
T = 512
NTILE = 4
TOK = 2048
NMEM = 256
DFF = 5632
EPS = 1e-6
NSLOT = 4
PW = 256
DBG = {"p1": 99}
FF_GROUPS = [(0, 12), (12, 12), (24, 12), (36, 8)]

G_MIX, G_XATTN, G_MEM, G_FFN, G_FINAL, G_PSCALE, G_SGU = 0, 16, 32, 48, 64, 80, 88
NG = 96


class Res:
    __slots__ = ("name", "w", "r")

    def __init__(self, name):
        self.name = name
        self.w = None
        self.r = {}


class Prog:
    ENGS = ("pe", "act", "dve", "pool", "sp")

    def __init__(self):
        self.ins = {e: [] for e in self.ENGS}
        self.dma_cnt = {}

    def op(self, eng, fn, reads=(), writes=(), dma=None):
        deps = []
        for t in reads:
            if t.w is not None:
                deps.append(t.w)
        for t in writes:
            if t.w is not None:
                deps.append(t.w)
            deps.extend(t.r.values())
        idx = len(self.ins[eng])
        if dma is not None:
            self.dma_cnt[dma] = self.dma_cnt.get(dma, 0) + 16
            ev = ("dma", dma, self.dma_cnt[dma])
        else:
            ev = ("eng", eng, idx)
        self.ins[eng].append(dict(fn=fn, deps=deps, ev=ev, dma=dma))
        for t in reads:
            t.r[(ev[0], ev[1])] = ev
        for t in writes:
            t.w = ev
            t.r = {}
        return ev

    def resolve(self):
        need = {e: set() for e in self.ENGS}
        for e in self.ENGS:
            seen = {}
            for ins in self.ins[e]:
                best = {}
                for d in ins["deps"]:
                    kind, key, val = d
                    if kind == "eng" and key == e and e == "pe":
                        continue
                    k = (kind, key)
                    if val > best.get(k, -1):
                        best[k] = val
                waits = []
                for k, val in best.items():
                    if val <= seen.get(k, -1):
                        continue
                    seen[k] = val
                    waits.append((k[0], k[1], val))
                    if k[0] == "eng":
                        need[k[1]].add(val)
                ins["waits"] = waits
        self.rank = {}
        for e in self.ENGS:
            self.rank[e] = {idx: i + 1 for i, idx in enumerate(sorted(need[e]))}
        self.need = need


def _build(ntile=NTILE, phases=3, record=None):
    if record is None:
        rec = []
        _build(ntile, phases, record=rec)
        sched_in = rec
    else:
        sched_in = None
    nc = bass.Bass("TRN2", target_bir_lowering=False)
    dr = {}

    def din(name, shape):
        dr[name] = nc.dram_tensor(name, list(shape), F32, kind="ExternalInput").ap()
        return dr[name]

    x = din("x", [TOK, D])
    xh = din("xh", [16, D])
    mem = din("mem", [NMEM, D])
    w_in = din("w_in", [D, 3072])
    pool_w = din("pool_w", [4, 256, 256])
    w_out = din("w_out", [D, D])
    w_q = din("w_q", [D, D])
    w_k = din("w_k", [D, D])
    w_v = din("w_v", [D, D])
    w_o = din("w_o", [D, D])
    w_gate = din("w_gate", [D, DFF])
    w_up = din("w_up", [D, DFF])
    w_down = din("w_down", [DFF, D])
    gv_d = din("gv", [128, NG])
    invc_d = din("invc", [128, 64])
    wsT_d = din("wsT", [128, 8 * 128])
    brow_d = din("brow", [1, 1024])
    ident_d = din("ident", [128, 128])
    y = nc.dram_tensor("y", [TOK, D], F32, kind="ExternalOutput").ap()
    W = dict(w_in=w_in, w_out=w_out, w_q=w_q, w_k=w_k, w_v=w_v, w_o=w_o,
             w_gate=w_gate, w_up=w_up, w_down=w_down)

    P = Prog()
    es = ExitStack()
    with es:
        def sb(name, shape, dt):
            return es.enter_context(nc.sbuf_tensor(name, list(shape), dt))

        xT = sb("xT", [128, KC, T], F32)
        hT = sb("hT", [128, KC, T], BF16)
        ym = sb("ym", [128, KC, T], BF16)
        vfm = sb("vfm", [128, 8, T], BF16)
        vtm = sb("vtm", [128, 4, 1024], BF16)
        xin = [sb(f"xin{i}", [128, 1024], F32) for i in range(2)]
        xout = [sb(f"xout{i}", [128, 1024], F32) for i in range(2)]
        sq = [sb(f"sq{i}", [128, T], BF16) for i in range(4)]
        aext = [sb(f"aext{i}", [128, 2, T + 16], F32) for i in range(2)]
        ptmp = [sb(f"ptmp{i}", [128, 2, T + 16], F32) for i in range(2)]
        pbf = [sb(f"pbf{i}", [128, 2, T], BF16) for i in range(2)]
        carry = sb("carry", [128, 8, 16], F32)
        expT = [sb(f"expT{i}", [128, 2, T], BF16) for i in range(2)]
        rden = [sb(f"rden{i}", [128, T], F32) for i in range(2)]
        sgt = [sb(f"sgt{i}", [128, T], F32) for i in range(2)]
        hid = sb("hid", [128, 12, T], BF16)
        lnv = sb("lnv", [128, T], F32)
        wsl = [sb(f"wsl{i}", [128, KC, PW], BF16) for i in range(NSLOT)]
        kT = sb("kT", [128, KC, NMEM], BF16)
        vmm = sb("vmm", [128, 2, D], BF16)
        gv = sb("gv_sb", [128, NG], F32)
        invc = sb("invc_sb", [128, 64], F32)
        wsT = sb("wsT_sb", [128, 8, 128], BF16)
        bmat = sb("bmat", [128, 1024], BF16)
        identf = sb("identf", [128, 128], F32)
        identb = sb("identb", [128, 128], BF16)
        onesb = sb("onesb", [128, 128], BF16)
        poolw = sb("poolw", [128, 4, 2, 256], BF16)
        xhT = sb("xhT", [128, KC, 16], F32)
        hhT = sb("hhT", [128, KC, 16], BF16)
        ps = es.enter_context(nc.psum_tensor("ps", [128, 8, 512], F32))

        R = lambda n: Res(n)
        r_xT = [R(f"xT{i}") for i in range(KC)]
        r_hT = [R(f"hT{i}") for i in range(KC)]
        r_ym = [R(f"ym{i}") for i in range(KC)]
        r_vfm = [R(f"vfm{i}") for i in range(8)]
        r_vtm = [R(f"vtm{i}") for i in range(4)]
        r_xin = [R("xin0"), R("xin1")]
        r_xout = [R("xout0"), R("xout1")]
        r_sq = [R(f"sq{i}") for i in range(4)]
        r_aext = [R("aext0"), R("aext1")]
        r_ptmp = [R("ptmp0"), R("ptmp1")]
        r_pbf = [R("pbf0"), R("pbf1")]
        r_carry = [R(f"carry{i}") for i in range(4)]
        r_expT = [R("expT0"), R("expT1")]
        r_rden = [R("rden0"), R("rden1")]
        r_sgt = [R("sgt0"), R("sgt1")]
        r_hid = [R(f"hid{i}") for i in range(12)]
        r_lnv = R("lnv")
        r_wsl = [R(f"wsl{i}") for i in range(NSLOT)]
        r_kT = R("kT")
        r_vmm = R("vmm")
        r_gv = R("gv")
        r_invc = R("invc")
        r_wsT = R("wsT")
        r_bmat = R("bmat")
        r_identf = R("identf")
        r_identb = R("identb")
        r_ones = R("ones")
        r_poolw = R("poolw")
        r_xhT = R("xhT")
        r_hhT = R("hhT")
        r_ps = [R(f"ps{i}") for i in range(8)]
        STAT = 7
        bank_ctr = [0]

        def next_bank():
            b = bank_ctr[0] % 7
            bank_ctr[0] += 1
            return b

        evac_ctr = [0]

        def evac_eng():
            evac_ctr[0] += 1
            return "act" if evac_ctr[0] % 2 else "dve"

        def copy_op(eng, out, in_, reads, writes, scale=None):
            if eng == "act":
                if scale is None:
                    P.op("act", lambda e, o=out, i=in_: e.activation(out=o, in_=i, func=AF.Copy),
                         reads=reads, writes=writes)
                else:
                    P.op("act", lambda e, o=out, i=in_, s=scale: e.activation(out=o, in_=i, func=AF.Copy, scale=s),
                         reads=reads, writes=writes)
            else:
                assert scale is None
                P.op("dve", lambda e, o=out, i=in_: e.tensor_copy(out=o, in_=i), reads=reads, writes=writes)

        P.op("sp", lambda e: e.dma_start(out=gv[:], in_=gv_d[:, :]), writes=[r_gv], dma="c_gv")
        P.op("sp", lambda e: e.dma_start(out=invc[:], in_=invc_d[:, :]), writes=[r_invc], dma="c_invc")
        P.op("sp", lambda e: e.dma_start(out=identf[:], in_=ident_d[:, :]), writes=[r_identf], dma="c_identf")
        P.op("pool", lambda e: e.dma_start(out=identb[:], in_=ident_d[:, :]), writes=[r_identb], dma="c_identb")
        P.op("pool", lambda e: e.dma_start(out=wsT[:], in_=wsT_d.rearrange("p (h t) -> p h t", h=8)),
             writes=[r_wsT], dma="c_wsT")
        P.op("pool", lambda e: e.dma_start(out=poolw[:], in_=pool_w.rearrange("g (ic p) d -> p g ic d", p=128)),
             writes=[r_poolw], dma="c_poolw")
        P.op("dve", lambda e: e.memset(onesb[:], 1.0), writes=[r_ones])
        P.op("dve", lambda e: e.memset(bmat[:], 0.0), writes=[r_bmat])
        P.op("pool", lambda e: e.dma_start(out=bmat[0:1, :], in_=brow_d[:, :]), writes=[r_bmat], dma="c_brow")
        P.op("dve", lambda e: e.memset(wsT[64:128, :, 0:64], 0.0), writes=[r_wsT])

        sched = sched_in if sched_in is not None else []
        wpos = {"issued": 0, "next": 0}

        def issue_loads(upto):
            while wpos["issued"] < min(upto, len(sched)):
                k = wpos["issued"]
                name, row0, nk, c0, ncols = sched[k]
                s = k % NSLOT
                src = W[name][row0:row0 + nk * 128, c0:c0 + ncols].rearrange("(kc p) n -> p kc n", p=128)
                dst = wsl[s][:, 0:nk, 0:ncols]
                P.op("pool", lambda e, d=dst, sr=src: e.dma_start(out=d, in_=sr),
                     writes=[r_wsl[s]], dma=f"w{s}")
                wpos["issued"] += 1

        def next_panel(name, row0, nk, c0):
            j = wpos["next"]
            if record is not None:
                record.append((name, row0, nk, c0, PW))
                wpos["next"] += 1
                return j % NSLOT
            assert sched[j][:4] == (name, row0, nk, c0), (sched[j], name, row0, nk, c0)
            issue_loads(j + NSLOT)
            wpos["next"] += 1
            return j % NSLOT

        def transpose_in(src, row0, nblk, dstT, r_dst, colbase, npart=128):
            cnt = 0
            for blk in range(nblk):
                for half in range(2):
                    b = cnt % 2
                    cnt += 1
                    P.op("sp", lambda e, o=xin[b][0:npart, :], i=src[row0 + blk * 128: row0 + blk * 128 + npart,
                                                                     half * 1024:(half + 1) * 1024]:
                         e.dma_start(out=o, in_=i), writes=[r_xin[b]], dma=f"xin{b}")
                    for cg in range(2):
                        bk = next_bank()
                        for j in range(4):
                            c = cg * 4 + j
                            P.op("pe", lambda e, o=ps[:, bk, j * npart:(j + 1) * npart],
                                 i=xin[b][0:npart, c * 128:(c + 1) * 128], idn=identf[0:npart, 0:npart]:
                                 e.transpose(out=o, in_=i, identity=idn),
                                 reads=[r_xin[b], r_identf], writes=[r_ps[bk]])
                        kc0 = half * 8 + cg * 4
                        c0 = colbase + blk * 128
                        copy_op(evac_eng(), dstT[:, kc0:kc0 + 4, c0:c0 + npart],
                                ps[:, bk, 0:4 * npart].rearrange("p (j t) -> p j t", j=4),
                                reads=[r_ps[bk]], writes=r_dst[kc0:kc0 + 4] if isinstance(r_dst, list) else [r_dst])

        def rmsnorm(srcT, r_src, goff, nch, ncols, dst, r_dst, inv_n, dst_f32=False):
            rs = lambda c: r_src[c] if isinstance(r_src, list) else r_src
            rd = lambda c: r_dst[c] if isinstance(r_dst, list) else r_dst
            for c in range(nch):
                q = c % 4
                P.op("act", lambda e, o=sq[q][:, 0:ncols], i=srcT[:, c, 0:ncols]:
                     e.activation(out=o, in_=i, func=AF.Square), reads=[rs(c)], writes=[r_sq[q]])
                P.op("pe", lambda e, o=ps[:, STAT, 0:ncols], r=sq[q][:, 0:ncols], st=(c == 0), sp_=(c == nch - 1):
                     e.matmul(o, lhsT=onesb[:], rhs=r, start=st, stop=sp_),
                     reads=[r_sq[q], r_ones], writes=[r_ps[STAT]])
            P.op("act", lambda e, o=lnv[:, 0:ncols], i=ps[:, STAT, 0:ncols]:
                 e.activation(out=o, in_=i, func=AF.Ln, scale=inv_n, bias=eps_ap[:, 0:1]),
                 reads=[r_ps[STAT], r_eps], writes=[r_lnv])
            P.op("act", lambda e, o=ps[:, STAT, 0:ncols], i=lnv[:, 0:ncols]:
                 e.activation(out=o, in_=i, func=AF.Exp, scale=-0.5),
                 reads=[r_lnv], writes=[r_ps[STAT]])
            for c in range(nch):
                P.op("dve", lambda e, o=dst[:, c, 0:ncols], i0=srcT[:, c, 0:ncols], s=gv[:, goff + c:goff + c + 1],
                     i1=ps[:, STAT, 0:ncols]:
                     e.scalar_tensor_tensor(out=o, in0=i0, scalar=s, in1=i1, op0=ALU.mult, op1=ALU.mult),
                     reads=[rs(c), r_gv, r_ps[STAT]], writes=[rd(c)])

        def proj(name, col0, nchunks, nk, row0, rhs_fn, r_rhs_fn, ncols, evac_fn):
            for pc in range(0, nchunks, PW // 128):
                s = next_panel(name, row0, nk, col0 + pc * 128)
                for cc in range(PW // 128):
                    oc = pc + cc
                    bk = next_bank()
                    for k in range(nk):
                        P.op("pe", lambda e, o=ps[:, bk, 0:ncols], l=wsl[s][:, k, cc * 128:(cc + 1) * 128],
                             r=rhs_fn(k), st=(k == 0), sp_=(k == nk - 1):
                             e.matmul(o, lhsT=l, rhs=r, start=st, stop=sp_),
                             reads=[r_wsl[s], r_rhs_fn(k)], writes=[r_ps[bk]])
                    evac_fn(oc, bk)

        def resid_add(oc, bk):
            P.op("dve", lambda e, o=xT[:, oc, :], i0=ps[:, bk, :], i1=xT[:, oc, :]:
                 e.tensor_tensor(out=o, in0=i0, in1=i1, op=ALU.add),
                 reads=[r_ps[bk]], writes=[r_xT[oc]])

        eps_ap = sb("eps_sb", [128, 1], F32)
        r_eps = R("eps")
        P.op("dve", lambda e: e.memset(eps_ap[:], EPS), writes=[r_eps])

        transpose_in(xh, 0, 1, xhT, r_xhT, 0, npart=16)
        rmsnorm(xhT, r_xhT, G_MIX, KC, 16, hhT, r_hhT, 1.0 / D)

        def halo_evac(oc, bk):
            copy_op("act", carry[:, oc, :], ps[:, bk, 0:16], reads=[r_ps[bk]], writes=[r_carry[oc // 2]])
        proj("w_in", 0, 8, KC, 0, lambda k: hhT[:, k, :], lambda k: r_hhT, 16, halo_evac)

        transpose_in(mem, 0, 2, xT, r_xT, 0)
        rmsnorm(xT, r_xT, G_MEM, KC, NMEM, hT, r_hT, 1.0 / D)

        def k_evac(oc, bk):
            copy_op(evac_eng(), kT[:, oc, :], ps[:, bk, 0:NMEM], reads=[r_ps[bk]], writes=[r_kT])
        proj("w_k", 0, KC, KC, 0, lambda k: hT[:, k, 0:NMEM], lambda k: r_hT[k], NMEM, k_evac)

        def v_evac(oc, bk):
            copy_op(evac_eng(), ym[:, oc, 0:NMEM], ps[:, bk, 0:NMEM], reads=[r_ps[bk]], writes=[r_ym[oc]])
        proj("w_v", 0, KC, KC, 0, lambda k: hT[:, k, 0:NMEM], lambda k: r_hT[k], NMEM, v_evac)
        for mc in range(2):
            for g8 in range(2):
                bk = next_bank()
                psb = ps[:, bk, :].bitcast(BF16)
                for j in range(8):
                    oc = g8 * 8 + j
                    P.op("pe", lambda e, o=psb[:, j * 128:(j + 1) * 128], i=ym[:, oc, mc * 128:(mc + 1) * 128]:
                         e.transpose(out=o, in_=i, identity=identb[:]),
                         reads=[r_ym[oc], r_identb], writes=[r_ps[bk]])
                copy_op(evac_eng(), vmm[:, mc, g8 * 1024:(g8 + 1) * 1024], psb[:, 0:1024],
                        reads=[r_ps[bk]], writes=[r_vmm])

        for ti in range(ntile):
            t0 = ti * T
            transpose_in(x, t0, 4, xT, r_xT, 0)

            if phases >= 1:
                rmsnorm(xT, r_xT, G_MIX, KC, T, hT, r_hT, 1.0 / D)

                def pool_group(g):
                    if DBG["p1"] < 2:
                        return
                    ab = g % 2
                    w = 2 ** (g + 1)
                    A = aext[ab]
                    P.op("dve", lambda e, o=A[:, :, 0:16], i=carry[:, 2 * g:2 * g + 2, :]: e.tensor_copy(out=o, in_=i),
                         reads=[r_carry[g]], writes=[r_aext[ab]])
                    P.op("dve", lambda e, o=carry[:, 2 * g:2 * g + 2, :], i=A[:, :, T:T + 16]: e.tensor_copy(out=o, in_=i),
                         reads=[r_aext[ab]], writes=[r_carry[g]])
                    src, r_srcs = A, [r_aext[ab]]
                    m = 1
                    step = 0
                    L = T + 16
                    while m < w:
                        dstb = ptmp[step % 2]
                        P.op("dve", lambda e, o=dstb[:, :, 2 * m - 1:L], i0=src[:, :, 2 * m - 1:L], i1=src[:, :, m - 1:L - m]:
                             e.tensor_tensor(out=o, in0=i0, in1=i1, op=ALU.add),
                             reads=r_srcs, writes=[r_ptmp[step % 2]])
                        src, r_srcs = dstb, [r_ptmp[step % 2]]
                        m *= 2
                        step += 1
                    pb = g % 2
                    P.op("dve", lambda e, o=pbf[pb][:, :, :], i0=src[:, :, 16:L], i1=A[:, :, 16:L]:
                         e.scalar_tensor_tensor(out=o, in0=i0, scalar=1.0 / w, in1=i1, op0=ALU.mult, op1=ALU.subtract),
                         reads=r_srcs + [r_aext[ab]], writes=[r_pbf[pb]])
                    if ti == 0:
                        other = ptmp[step % 2]
                        for cc in range(2):
                            P.op("dve", lambda e, o=other[:, cc, 0:16], i0=src[:, cc, 16:32], i1=invc[:, g * 16:(g + 1) * 16]:
                                 e.tensor_tensor(out=o, in0=i0, in1=i1, op=ALU.mult),
                                 reads=r_srcs + [r_invc], writes=[r_ptmp[step % 2]])
                            P.op("dve", lambda e, o=pbf[pb][:, cc, 0:16], i0=other[:, cc, 0:16], i1=A[:, cc, 16:32]:
                                 e.tensor_tensor(out=o, in0=i0, in1=i1, op=ALU.subtract),
                                 reads=[r_ptmp[step % 2], r_aext[ab]], writes=[r_pbf[pb]])
                    if DBG["p1"] < 3:
                        return
                    for oc in range(2):
                        bk = next_bank()
                        for ic in range(2):
                            P.op("pe", lambda e, o=ps[:, bk, :], l=poolw[:, g, ic, oc * 128:(oc + 1) * 128],
                                 r=pbf[pb][:, ic, :], st=(ic == 0), sp_=(ic == 1):
                                 e.matmul(o, lhsT=l, rhs=r, start=st, stop=sp_),
                                 reads=[r_poolw, r_pbf[pb]], writes=[r_ps[bk]])
                        ch = 2 * g + oc
                        copy_op("act", ym[:, ch, :], ps[:, bk, :], reads=[r_ps[bk]], writes=[r_ym[ch]],
                                scale=gv[:, G_PSCALE + ch:G_PSCALE + ch + 1])

                def win_evac(oc, bk):
                    if oc < 8:
                        g = oc // 2
                        copy_op("act", aext[g % 2][:, oc % 2, 16:T + 16], ps[:, bk, :],
                                reads=[r_ps[bk]], writes=[r_aext[g % 2]])
                        if oc % 2 == 1:
                            pool_group(g)
                    elif oc < 16:
                        copy_op("act", ym[:, oc, :], ps[:, bk, :], reads=[r_ps[bk]], writes=[r_ym[oc]])
                    else:
                        j = oc - 16
                        q = j % 4
                        if j > 0 and DBG["p1"] >= 4:
                            vstat_mm(j - 1)
                        P.op("act", lambda e, o=sq[q][:, :], i=ps[:, bk, :]: e.activation(out=o, in_=i, func=AF.Square),
                             reads=[r_ps[bk]], writes=[r_sq[q]])
                        copy_op("act", vfm[:, j, :], ps[:, bk, :], reads=[r_ps[bk]], writes=[r_vfm[j]])

                def vstat_mm(j):
                    q = j % 4
                    P.op("pe", lambda e, r=sq[q][:, :], st=(j == 0), sp_=(j == 7):
                         e.matmul(ps[:, STAT, :], lhsT=onesb[:], rhs=r, start=st, stop=sp_),
                         reads=[r_sq[q], r_ones], writes=[r_ps[STAT]])
                proj("w_in", 0, 24, KC, 0, lambda k: hT[:, k, :], lambda k: r_hT[k], T, win_evac)
                if DBG["p1"] >= 4:
                    vstat_mm(7)

                if DBG["p1"] >= 4:
                  P.op("act", lambda e: e.activation(out=lnv[:], in_=ps[:, STAT, :], func=AF.Ln, scale=1.0 / 1024,
                                                   bias=eps_ap[:, 0:1]),
                     reads=[r_ps[STAT], r_eps], writes=[r_lnv])
                  P.op("act", lambda e: e.activation(out=ps[:, STAT, :], in_=lnv[:], func=AF.Exp, scale=-0.5),
                     reads=[r_lnv], writes=[r_ps[STAT]])
                for j in range(8 if DBG["p1"] >= 4 else 0):
                    P.op("dve", lambda e, o=vfm[:, j, :], s=gv[:, G_SGU + j:G_SGU + j + 1]:
                         e.scalar_tensor_tensor(out=o, in0=o, scalar=s, in1=ps[:, STAT, :], op0=ALU.mult, op1=ALU.mult),
                         reads=[r_gv, r_ps[STAT]], writes=[r_vfm[j]])
                for blk in range(4 if DBG["p1"] >= 5 else 0):
                    bk = next_bank()
                    psb = ps[:, bk, :].bitcast(BF16)
                    for j in range(8):
                        P.op("pe", lambda e, o=psb[:, j * 128:(j + 1) * 128], i=vfm[:, j, blk * 128:(blk + 1) * 128]:
                             e.transpose(out=o, in_=i, identity=identb[:]),
                             reads=[r_vfm[j], r_identb], writes=[r_ps[bk]])
                    copy_op(evac_eng(), vtm[:, blk, :], psb[:, 0:1024], reads=[r_ps[bk]], writes=[r_vtm[blk]])
                for h in range(8 if DBG["p1"] >= 6 else 0):
                    bk = next_bank()
                    for blk in range(4):
                        o = ps[:, bk, blk * 128:(blk + 1) * 128]
                        P.op("pe", lambda e, o=o, l=vtm[:, blk, h * 128:(h + 1) * 128], r=wsT[:, h, :]:
                             e.matmul(o, lhsT=l, rhs=r, start=True, stop=False),
                             reads=[r_vtm[blk], r_wsT], writes=[r_ps[bk]])
                        P.op("pe", lambda e, o=o, r=bmat[:, h * 128:(h + 1) * 128]:
                             e.matmul(o, lhsT=onesb[:], rhs=r, start=False, stop=True),
                             reads=[r_bmat, r_ones], writes=[r_ps[bk]])
                    P.op("dve", lambda e, o=ym[:, 8 + h, :], i0=ps[:, bk, :]:
                         e.tensor_tensor(out=o, in0=i0, in1=o, op=ALU.mult),
                         reads=[r_ps[bk]], writes=[r_ym[8 + h]])
                if DBG["p1"] >= 7:
                    proj("w_out", 0, KC, KC, 0, lambda k: ym[:, k, :], lambda k: r_ym[k], T, resid_add)

            if phases >= 2:
                rmsnorm(xT, r_xT, G_XATTN, KC, T, hT, r_hT, 1.0 / D)

                def q_evac(oc, bk):
                    copy_op("act", ym[:, oc, :], ps[:, bk, :], reads=[r_ps[bk]], writes=[r_ym[oc]],
                            scale=float(512 ** -0.5))
                proj("w_q", 0, KC, KC, 0, lambda k: hT[:, k, :], lambda k: r_hT[k], T, q_evac)
                for h in range(4):
                    eb = h % 2
                    for mc in range(2):
                        bk = next_bank()
                        for dc in range(4):
                            ch = 4 * h + dc
                            P.op("pe", lambda e, o=ps[:, bk, :], l=kT[:, ch, mc * 128:(mc + 1) * 128], r=ym[:, ch, :],
                                 st=(dc == 0), sp_=(dc == 3): e.matmul(o, lhsT=l, rhs=r, start=st, stop=sp_),
                                 reads=[r_kT, r_ym[ch]], writes=[r_ps[bk]])
                        P.op("act", lambda e, o=expT[eb][:, mc, :], i=ps[:, bk, :]: e.activation(out=o, in_=i, func=AF.Exp),
                             reads=[r_ps[bk]], writes=[r_expT[eb]])
                    bk = next_bank()
                    for mc in range(2):
                        P.op("pe", lambda e, o=ps[:, bk, :], r=expT[eb][:, mc, :], st=(mc == 0), sp_=(mc == 1):
                             e.matmul(o, lhsT=onesb[:], rhs=r, start=st, stop=sp_),
                             reads=[r_expT[eb], r_ones], writes=[r_ps[bk]])
                    P.op("dve", lambda e, o=rden[eb][:], i=ps[:, bk, :]: e.reciprocal(out=o, in_=i),
                         reads=[r_ps[bk]], writes=[r_rden[eb]])
                    for dc in range(4):
                        ch = 4 * h + dc
                        bk = next_bank()
                        for mc in range(2):
                            P.op("pe", lambda e, o=ps[:, bk, :], l=vmm[:, mc, ch * 128:(ch + 1) * 128], r=expT[eb][:, mc, :],
                                 st=(mc == 0), sp_=(mc == 1): e.matmul(o, lhsT=l, rhs=r, start=st, stop=sp_),
                                 reads=[r_vmm, r_expT[eb]], writes=[r_ps[bk]])
                        P.op("dve", lambda e, o=hT[:, ch, :], i0=ps[:, bk, :], i1=rden[eb][:]:
                             e.tensor_tensor(out=o, in0=i0, in1=i1, op=ALU.mult),
                             reads=[r_ps[bk], r_rden[eb]], writes=[r_hT[ch]])
                proj("w_o", 0, KC, KC, 0, lambda k: hT[:, k, :], lambda k: r_hT[k], T, resid_add)

            if phases >= 3:
                rmsnorm(xT, r_xT, G_FFN, KC, T, hT, r_hT, 1.0 / D)
                for (f0, nf) in FF_GROUPS:
                    for pc in range(0, nf, PW // 128):
                        c0 = (f0 + pc) * 128
                        sg_ = next_panel("w_gate", 0, KC, c0)
                        for cc in range(PW // 128):
                            bg = next_bank()
                            for k in range(KC):
                                P.op("pe", lambda e, o=ps[:, bg, :], l=wsl[sg_][:, k, cc * 128:(cc + 1) * 128], r=hT[:, k, :],
                                     st=(k == 0), sp_=(k == KC - 1): e.matmul(o, lhsT=l, rhs=r, start=st, stop=sp_),
                                     reads=[r_wsl[sg_], r_hT[k]], writes=[r_ps[bg]])
                            P.op("act", lambda e, o=sgt[cc][:], i=ps[:, bg, :]: e.activation(out=o, in_=i, func=AF.Silu),
                                 reads=[r_ps[bg]], writes=[r_sgt[cc]])
                        su_ = next_panel("w_up", 0, KC, c0)
                        for cc in range(PW // 128):
                            j = pc + cc
                            bu = next_bank()
                            for k in range(KC):
                                P.op("pe", lambda e, o=ps[:, bu, :], l=wsl[su_][:, k, cc * 128:(cc + 1) * 128], r=hT[:, k, :],
                                     st=(k == 0), sp_=(k == KC - 1): e.matmul(o, lhsT=l, rhs=r, start=st, stop=sp_),
                                     reads=[r_wsl[su_], r_hT[k]], writes=[r_ps[bu]])
                            P.op("dve", lambda e, o=hid[:, j, :], i0=ps[:, bu, :], i1=sgt[cc][:]:
                                 e.tensor_tensor(out=o, in0=i0, in1=i1, op=ALU.mult),
                                 reads=[r_ps[bu], r_sgt[cc]], writes=[r_hid[j]])
                    proj("w_down", 0, KC, nf, f0 * 128, lambda k: hid[:, k, :], lambda k: r_hid[k], T, resid_add)

            rmsnorm(xT, r_xT, G_FINAL, KC, T, xT, r_xT, 1.0 / D)
            cnt = 0
            for blk in range(4):
                for half in range(2):
                    ob = cnt % 2
                    cnt += 1
                    for cg in range(2):
                        bk = next_bank()
                        for j in range(4):
                            kc = half * 8 + cg * 4 + j
                            P.op("pe", lambda e, o=ps[:, bk, j * 128:(j + 1) * 128], i=xT[:, kc, blk * 128:(blk + 1) * 128]:
                                 e.transpose(out=o, in_=i, identity=identf[:]),
                                 reads=[r_xT[kc], r_identf], writes=[r_ps[bk]])
                        copy_op(evac_eng(), xout[ob][:, cg * 512:(cg + 1) * 512], ps[:, bk, :],
                                reads=[r_ps[bk]], writes=[r_xout[ob]])
                    P.op("sp", lambda e, i=xout[ob][:], o=y[t0 + blk * 128:t0 + (blk + 1) * 128,
                                                           half * 1024:(half + 1) * 1024]:
                         e.dma_start(out=o, in_=i), reads=[r_xout[ob]], dma=f"st{ob}")

        if record is not None:
            return None
        assert wpos["next"] == len(sched), (wpos, len(sched))
        P.op("sp", lambda e: None, writes=[r_xout[0], r_xout[1]])

        P.resolve()
        sems = {e: es.enter_context(nc.semaphore(f"s_{e}")) for e in ("pe", "act", "dve")}
        dsem = {k: es.enter_context(nc.semaphore(f"d_{k}")) for k in P.dma_cnt}

        def body(engname):
            def run(e):
                rank = P.rank.get(engname, {})
                for idx, ins in enumerate(P.ins[engname]):
                    for (kind, key, val) in ins["waits"]:
                        if kind == "eng":
                            e.wait_ge(sems[key], P.rank[key][val])
                        else:
                            e.wait_ge(dsem[key], val)
                    r = ins["fn"](e)
                    if ins["dma"] is not None:
                        r.then_inc(dsem[ins["dma"]], 16)
                    elif idx in rank:
                        r.then_inc(sems[engname], 1)
            return run

        with nc.Block() as block:
            block.sync(body("sp"))
            block.gpsimd(body("pool"))
            block.tensor(body("pe"))
            block.scalar(body("act"))
            block.vector(body("dve"))
    return nc


_NC_CACHE = {}


def _get_nc():
    if "nc" not in _NC_CACHE:
        _NC_CACHE["nc"] = _build()
    return _NC_CACHE["nc"]


def kernel(x, mem, norm_mix_g, w_in, pool_w, pool_scale, sgu_norm_g, w_spatial, b_spatial, w_out,
           norm_xattn_g, norm_mem_g, w_q, w_k, w_v, w_o, norm_ffn_g, w_gate, w_up, w_down, final_norm_g):
    f = lambda a: np.ascontiguousarray(np.asarray(a, dtype=np.float32))
    x = f(x)
    mem = f(mem)
    B, S, _ = x.shape

    def fm(vec):
        v = f(vec).reshape(-1)
        return v.reshape(-1, 128).T

    gvh = np.ascontiguousarray(np.concatenate(
        [fm(norm_mix_g[0]), fm(norm_xattn_g[0]), fm(norm_mem_g[0]), fm(norm_ffn_g[0]), fm(final_norm_g),
         fm(pool_scale[0]), fm(sgu_norm_g[0])], axis=1).astype(np.float32))
    assert gvh.shape == (128, NG)
    wsT_h = np.ascontiguousarray(np.transpose(f(w_spatial)[0], (2, 0, 1)).reshape(128, 8 * 128))
    brow_h = f(b_spatial)[0].reshape(1, 1024)
    ident_h = np.eye(128, dtype=np.float32)
    shared = dict(
        w_in=f(w_in)[0], pool_w=f(pool_w)[0], w_out=f(w_out)[0], w_q=f(w_q)[0], w_k=f(w_k)[0], w_v=f(w_v)[0],
        w_o=f(w_o)[0], w_gate=f(w_gate)[0], w_up=f(w_up)[0], w_down=f(w_down)[0],
        gv=gvh, wsT=wsT_h, brow=brow_h, ident=ident_h)
    in_maps = []
    for c in range(8):
        b, half = c // 2, c % 2
        s0 = half * TOK
        xc = x[b, s0:s0 + TOK]
        if half == 0:
            xhc = np.zeros((16, D), np.float32)
        else:
            xhc = x[b, s0 - 16:s0]
        pos = np.arange(s0 + 1, s0 + 17, dtype=np.float32)
        tab = np.concatenate([1.0 / np.minimum(pos, float(w)) for w in (2, 4, 8, 16)]).astype(np.float32)
        m = dict(shared)
        m.update(x=np.ascontiguousarray(xc), xh=np.ascontiguousarray(xhc), mem=np.ascontiguousarray(mem[b]),
                 invc=np.ascontiguousarray(np.broadcast_to(tab[None, :], (128, 64))))
        in_maps.append(m)
    nc = _get_nc()
    res = run_bass_kernel_spmd(nc, in_maps, core_ids=list(range(8)))
    out = np.empty((B, S, D), np.float32)
    for c in range(8):
        b, half = c // 2, c % 2
        out[b, half * TOK:(half + 1) * TOK] = res.results[c]["y"]
    return out
```

```python
import numpy as np
from contextlib import ExitStack

import concourse.bass as bass
import concourse.mybir as mybir
from concourse.bass_utils import run_bass_kernel_spmd

F32 = mybir.dt.float32
BF16 = mybir.dt.bfloat16
AF = mybir.ActivationFunctionType
ALU = mybir.AluOpType

D = 2048
KC = 16
T = 512
NTILE = 4
TOK = 2048
NMEM = 256
DFF = 5632
EPS = 1e-6
NSLOT = 4
PW = 256
FF_GROUPS = [(0, 12), (12, 12), (24, 12), (36, 8)]

G_MIX, G_XATTN, G_MEM, G_FFN, G_FINAL, G_PSCALE, G_SGU = 0, 16, 32, 48, 64, 80, 88
NG = 96


class Res:
    __slots__ = ("name", "w", "r")

    def __init__(self, name):
        self.name = name
        self.w = None
        self.r = {}


class Prog:
    ENGS = ("pe", "act", "dve", "pool", "sp")

    def __init__(self):
        self.ins = {e: [] for e in self.ENGS}
        self.dma_cnt = {}

    def op(self, eng, fn, reads=(), writes=(), dma=None):
        deps = []
        for t in reads:
            if t.w is not None:
                deps.append(t.w)
        for t in writes:
            if t.w is not None:
                deps.append(t.w)
            deps.extend(t.r.values())
        idx = len(self.ins[eng])
        if dma is not None:
            self.dma_cnt[dma] = self.dma_cnt.get(dma, 0) + 16
            ev = ("dma", dma, self.dma_cnt[dma])
        else:
            ev = ("eng", eng, idx)
        self.ins[eng].append(dict(fn=fn, deps=deps, ev=ev, dma=dma))
        for t in reads:
            t.r[(ev[0], ev[1])] = ev
        for t in writes:
            t.w = ev
            t.r = {}
        return ev

    def resolve(self):
        need = {e: set() for e in self.ENGS}
        for e in self.ENGS:
            seen = {}
            for ins in self.ins[e]:
                best = {}
                for d in ins["deps"]:
                    kind, key, val = d
                    if kind == "eng" and key == e and e == "pe":
                        continue
                    k = (kind, key)
                    if val > best.get(k, -1):
                        best[k] = val
                waits = []
                for k, val in best.items():
                    if val <= seen.get(k, -1):
                        continue
                    seen[k] = val
                    waits.append((k[0], k[1], val))
                    if k[0] == "eng":
                        need[k[1]].add(val)
                ins["waits"] = waits
        self.rank = {}
        for e in self.ENGS:
            self.rank[e] = {idx: i + 1 for i, idx in enumerate(sorted(need[e]))}
        self.need = need


def _build(ntile=NTILE, phases=3, record=None):
    if record is None:
        rec = []
        _build(ntile, phases, record=rec)
        sched_in = rec
    else:
        sched_in = None
    nc = bass.Bass("TRN2", target_bir_lowering=False)
    dr = {}

    def din(name, shape):
        dr[name] = nc.dram_tensor(name, list(shape), F32, kind="ExternalInput").ap()
        return dr[name]

    x = din("x", [TOK, D])
    xh = din("xh", [16, D])
    mem = din("mem", [NMEM, D])
    w_in = din("w_in", [D, 3072])
    pool_w = din("pool_w", [4, 256, 256])
    w_out = din("w_out", [D, D])
    w_q = din("w_q", [D, D])
    w_k = din("w_k", [D, D])
    w_v = din("w_v", [D, D])
    w_o = din("w_o", [D, D])
    w_gate = din("w_gate", [D, DFF])
    w_up = din("w_up", [D, DFF])
    w_down = din("w_down", [DFF, D])
    gv_d = din("gv", [128, NG])
    invc_d = din("invc", [128, 64])
    wsT_d = din("wsT", [128, 8 * 128])
    brow_d = din("brow", [1, 1024])
    ident_d = din("ident", [128, 128])
    gbc_d = din("gbc", [128, D])
    y = nc.dram_tensor("y", [TOK, D], F32, kind="ExternalOutput").ap()
    W = dict(w_in=w_in, w_out=w_out, w_q=w_q, w_k=w_k, w_v=w_v, w_o=w_o,
             w_gate=w_gate, w_up=w_up, w_down=w_down)

    P = Prog()
    es = ExitStack()
    with es:
        def sb(name, shape, dt):
            return es.enter_context(nc.sbuf_tensor(name, list(shape), dt))

        xT = sb("xT", [128, KC, T], F32)
        hT = sb("hT", [128, KC, T], BF16)
        ym = sb("ym", [128, KC, T], BF16)
        vfm = sb("vfm", [128, 8, T], BF16)
        xin = [sb(f"xin{i}", [128, 1024], F32) for i in range(2)]
        xout = [sb(f"xout{i}", [128, 1024], F32) for i in range(2)]
        sq = [sb(f"sq{i}", [128, T], BF16) for i in range(4)]
        aext = [sb(f"aext{i}", [128, 2, T + 16], F32) for i in range(2)]
        ptmp = [sb(f"ptmp{i}", [128, 2, T + 16], F32) for i in range(2)]
        pbf = [sb(f"pbf{i}", [128, 2, T], BF16) for i in range(2)]
        carry = sb("carry", [128, 8, 16], F32)
        expT = [sb(f"expT{i}", [128, 2, T], BF16) for i in range(2)]
        rden = [sb(f"rden{i}", [128, T], F32) for i in range(2)]
        sgt = [sb(f"sgt{i}", [128, T], F32) for i in range(2)]
        hid = sb("hid", [128, 12, T], BF16)
        vtm = hid[:, 0:8, :].rearrange("p (b two) t -> p b (two t)", two=2)
        lnv = sb("lnv", [128, T], F32)
        wsl = [sb(f"wsl{i}", [128, KC, PW], BF16) for i in range(NSLOT)]
        kT = sb("kT", [128, KC, NMEM], BF16)
        vmm = sb("vmm", [128, 2, D], BF16)
        gv = sb("gv_sb", [128, NG], F32)
        invc = sb("invc_sb", [128, 64], F32)
        wsT = sb("wsT_sb", [128, 8, 128], BF16)
        bmat = sb("bmat", [128, 1024], BF16)
        identf = sb("identf", [128, 128], F32)
        identb = sb("identb", [128, 128], BF16)
        onesb = sb("onesb", [128, 128], BF16)
        poolw = sb("poolw", [128, 4, 2, 256], BF16)
        xhT = sb("xhT", [128, KC, 16], F32)
        gbc = sb("gbc_sb", [128, D], F32)
        lnc = sb("lnc", [128, 4], F32)
        rstdc = sb("rstdc", [128, 4], F32)
        hhT = sb("hhT", [128, KC, 16], BF16)
        ps = es.enter_context(nc.psum_tensor("ps", [128, 8, 512], F32))

        R = lambda n: Res(n)
        r_xT = [R(f"xT{i}") for i in range(KC)]
        r_hT = [R(f"hT{i}") for i in range(KC)]
        r_ym = [R(f"ym{i}") for i in range(KC)]
        r_vfm = [R(f"vfm{i}") for i in range(8)]
        r_xin = [R("xin0"), R("xin1")]
        r_xout = [R("xout0"), R("xout1")]
        r_sq = [R(f"sq{i}") for i in range(4)]
        r_aext = [R("aext0"), R("aext1")]
        r_ptmp = [R("ptmp0"), R("ptmp1")]
        r_pbf = [R("pbf0"), R("pbf1")]
        r_carry = [R(f"carry{i}") for i in range(4)]
        r_expT = [R("expT0"), R("expT1")]
        r_rden = [R("rden0"), R("rden1")]
        r_sgt = [R("sgt0"), R("sgt1")]
        r_hid = [R(f"hid{i}") for i in range(12)]
        r_lnv = R("lnv")
        r_wsl = [R(f"wsl{i}") for i in range(NSLOT)]
        r_kT = R("kT")
        r_vmm = R("vmm")
        r_gv = R("gv")
        r_invc = R("invc")
        r_wsT = R("wsT")
        r_bmat = R("bmat")
        r_identf = R("identf")
        r_identb = R("identb")
        r_ones = R("ones")
        r_poolw = R("poolw")
        r_xhT = R("xhT")
        r_gbc = R("gbc")
        r_lnc = R("lnc")
        r_rstdc = R("rstdc")
        r_hhT = R("hhT")
        r_ps = [R(f"ps{i}") for i in range(8)]
        STAT = 7
        bank_ctr = [0]

        def next_bank():
            b = bank_ctr[0] % 7
            bank_ctr[0] += 1
            return b

        evac_ctr = [0]

        def evac_eng():
            evac_ctr[0] += 1
            return "act" if evac_ctr[0] % 2 else "dve"

        def copy_op(eng, out, in_, reads, writes, scale=None):
            if eng == "act":
                if scale is None:
                    P.op("act", lambda e, o=out, i=in_: e.activation(out=o, in_=i, func=AF.Copy),
                         reads=reads, writes=writes)
                else:
                    P.op("act", lambda e, o=out, i=in_, s=scale: e.activation(out=o, in_=i, func=AF.Copy, scale=s),
                         reads=reads, writes=writes)
            else:
                assert scale is None
                P.op("dve", lambda e, o=out, i=in_: e.tensor_copy(out=o, in_=i), reads=reads, writes=writes)

        P.op("sp", lambda e: e.dma_start(out=gv[:], in_=gv_d[:, :]), writes=[r_gv], dma="c_gv")
        P.op("sp", lambda e: e.dma_start(out=invc[:], in_=invc_d[:, :]), writes=[r_invc], dma="c_invc")
        P.op("sp", lambda e: e.dma_start(out=identf[:], in_=ident_d[:, :]), writes=[r_identf], dma="c_identf")
        P.op("sp", lambda e: e.dma_start(out=gbc[:], in_=gbc_d[:, :]), writes=[r_gbc], dma="c_gbc")
        P.op("pool", lambda e: e.dma_start(out=identb[:], in_=ident_d[:, :]), writes=[r_identb], dma="c_identb")
        P.op("pool", lambda e: e.dma_start(out=wsT[:], in_=wsT_d.rearrange("p (h t) -> p h t", h=8)),
             writes=[r_wsT], dma="c_wsT")
        P.op("pool", lambda e: e.dma_start(out=poolw[:], in_=pool_w.rearrange("g (ic p) d -> p g ic d", p=128)),
             writes=[r_poolw], dma="c_poolw")
        P.op("dve", lambda e: e.memset(onesb[:], 1.0), writes=[r_ones])
        P.op("dve", lambda e: e.memset(bmat[:], 0.0), writes=[r_bmat])
        P.op("pool", lambda e: e.dma_start(out=bmat[0:1, :], in_=brow_d[:, :]), writes=[r_bmat], dma="c_brow")
        P.op("dve", lambda e: e.memset(wsT[64:128, :, 0:64], 0.0), writes=[r_wsT])

        sched = sched_in if sched_in is not None else []
        wpos = {"issued": 0, "next": 0}

        def issue_loads(upto):
            while wpos["issued"] < min(upto, len(sched)):
                k = wpos["issued"]
                name, row0, nk, c0, ncols = sched[k]
                s = k % NSLOT
                src = W[name][row0:row0 + nk * 128, c0:c0 + ncols].rearrange("(kc p) n -> p kc n", p=128)
                dst = wsl[s][:, 0:nk, 0:ncols]
                P.op("pool", lambda e, d=dst, sr=src: e.dma_start(out=d, in_=sr),
                     writes=[r_wsl[s]], dma=f"w{s}")
                wpos["issued"] += 1

        def next_panel(name, row0, nk, c0):
            j = wpos["next"]
            if record is not None:
                record.append((name, row0, nk, c0, PW))
                wpos["next"] += 1
                return j % NSLOT
            assert sched[j][:4] == (name, row0, nk, c0), (sched[j], name, row0, nk, c0)
            issue_loads(j + NSLOT)
            wpos["next"] += 1
            return j % NSLOT

        preissued = {}

        def xload(src, row0, i, npart):
            blk, half = divmod(i, 2)
            b = i % 2
            P.op("sp", lambda e, o=xin[b][0:npart, :], i_=src[row0 + blk * 128: row0 + blk * 128 + npart,
                                                              half * 1024:(half + 1) * 1024]:
                 e.dma_start(out=o, in_=i_), writes=[r_xin[b]], dma=f"xin{b}")

        def transpose_in(src, row0, nblk, dstT, r_dst, colbase, npart=128, key=None, nxt=None):
            n = nblk * 2
            pre = preissued.pop(key, 0) if key is not None else 0
            for i in range(pre, min(2, n)):
                xload(src, row0, i, npart)
            for i in range(n):
                blk, half = divmod(i, 2)
                b = i % 2
                for cg in range(2):
                    bk = next_bank()
                    for j in range(4):
                        c = cg * 4 + j
                        P.op("pe", lambda e, o=ps[:, bk, j * npart:(j + 1) * npart],
                             i_=xin[b][0:npart, c * 128:(c + 1) * 128], idn=identf[0:npart, 0:npart]:
                             e.transpose(out=o, in_=i_, identity=idn),
                             reads=[r_xin[b], r_identf], writes=[r_ps[bk]])
                    kc0 = half * 8 + cg * 4
                    c0 = colbase + blk * 128
                    copy_op(evac_eng(), dstT[:, kc0:kc0 + 4, c0:c0 + npart],
                            ps[:, bk, 0:4 * npart].rearrange("p (j t) -> p j t", j=4),
                            reads=[r_ps[bk]], writes=r_dst[kc0:kc0 + 4] if isinstance(r_dst, list) else [r_dst])
                if i + 2 < n:
                    xload(src, row0, i + 2, npart)
                elif nxt is not None:
                    nsrc, nrow0, nnpart, nkey = nxt
                    xload(nsrc, nrow0, i + 2 - n, nnpart)
                    preissued[nkey] = preissued.get(nkey, 0) + 1

        def rmsnorm(srcT, r_src, goff, nch, ncols, dst, r_dst, inv_n, dst_f32=False):
            rs = lambda c: r_src[c] if isinstance(r_src, list) else r_src
            rd = lambda c: r_dst[c] if isinstance(r_dst, list) else r_dst
            for c in range(nch):
                q = c % 4
                P.op("act", lambda e, o=sq[q][:, 0:ncols], i=srcT[:, c, 0:ncols]:
                     e.activation(out=o, in_=i, func=AF.Square), reads=[rs(c)], writes=[r_sq[q]])
                P.op("pe", lambda e, o=ps[:, STAT, 0:ncols], r=sq[q][:, 0:ncols], st=(c == 0), sp_=(c == nch - 1):
                     e.matmul(o, lhsT=onesb[:], rhs=r, start=st, stop=sp_),
                     reads=[r_sq[q], r_ones], writes=[r_ps[STAT]])
            P.op("act", lambda e, o=lnv[:, 0:ncols], i=ps[:, STAT, 0:ncols]:
                 e.activation(out=o, in_=i, func=AF.Ln, scale=inv_n, bias=eps_ap[:, 0:1]),
                 reads=[r_ps[STAT], r_eps], writes=[r_lnv])
            P.op("act", lambda e, o=ps[:, STAT, 0:ncols], i=lnv[:, 0:ncols]:
                 e.activation(out=o, in_=i, func=AF.Exp, scale=-0.5),
                 reads=[r_lnv], writes=[r_ps[STAT]])
            for c in range(nch):
                P.op("dve", lambda e, o=dst[:, c, 0:ncols], i0=srcT[:, c, 0:ncols], s=gv[:, goff + c:goff + c + 1],
                     i1=ps[:, STAT, 0:ncols]:
                     e.scalar_tensor_tensor(out=o, in0=i0, scalar=s, in1=i1, op0=ALU.mult, op1=ALU.mult),
                     reads=[rs(c), r_gv, r_ps[STAT]], writes=[rd(c)])

        def proj(name, col0, nchunks, nk, row0, rhs_fn, r_rhs_fn, ncols, evac_fn):
            for pc in range(0, nchunks, PW // 128):
                s = next_panel(name, row0, nk, col0 + pc * 128)
                for cc in range(PW // 128):
                    oc = pc + cc
                    bk = next_bank()
                    for k in range(nk):
                        P.op("pe", lambda e, o=ps[:, bk, 0:ncols], l=wsl[s][:, k, cc * 128:(cc + 1) * 128],
                             r=rhs_fn(k), st=(k == 0), sp_=(k == nk - 1):
                             e.matmul(o, lhsT=l, rhs=r, start=st, stop=sp_),
                             reads=[r_wsl[s], r_rhs_fn(k)], writes=[r_ps[bk]])
                    evac_fn(oc, bk)

        def resid_add(oc, bk):
            P.op("dve", lambda e, o=xT[:, oc, :], i0=ps[:, bk, :], i1=xT[:, oc, :]:
                 e.tensor_tensor(out=o, in0=i0, in1=i1, op=ALU.add),
                 reads=[r_ps[bk]], writes=[r_xT[oc]])

        eps_ap = sb("eps_sb", [128, 1], F32)
        r_eps = R("eps")
        P.op("dve", lambda e: e.memset(eps_ap[:], EPS), writes=[r_eps])

        transpose_in(xh, 0, 1, xhT, r_xhT, 0, npart=16, key="halo", nxt=(mem, 0, 128, "mem"))
        rmsnorm(xhT, r_xhT, G_MIX, KC, 16, hhT, r_hhT, 1.0 / D)

        def halo_evac(oc, bk):
            copy_op("act", carry[:, oc, :], ps[:, bk, 0:16], reads=[r_ps[bk]], writes=[r_carry[oc // 2]])
        proj("w_in", 0, 8, KC, 0, lambda k: hhT[:, k, :], lambda k: r_hhT, 16, halo_evac)

        transpose_in(mem, 0, 2, xT, r_xT, 0, key="mem", nxt=(x, 0, 128, "t0"))
        rmsnorm(xT, r_xT, G_MEM, KC, NMEM, hT, r_hT, 1.0 / D)

        def k_evac(oc, bk):
            copy_op(evac_eng(), kT[:, oc, :], ps[:, bk, 0:NMEM], reads=[r_ps[bk]], writes=[r_kT])
        proj("w_k", 0, KC, KC, 0, lambda k: hT[:, k, 0:NMEM], lambda k: r_hT[k], NMEM, k_evac)

        def v_evac(oc, bk):
            copy_op(evac_eng(), ym[:, oc, 0:NMEM], ps[:, bk, 0:NMEM], reads=[r_ps[bk]], writes=[r_ym[oc]])
        proj("w_v", 0, KC, KC, 0, lambda k: hT[:, k, 0:NMEM], lambda k: r_hT[k], NMEM, v_evac)
        for mc in range(2):
            for g8 in range(2):
                bk = next_bank()
                psb = ps[:, bk, :].bitcast(BF16)
                for j in range(8):
                    oc = g8 * 8 + j
                    P.op("pe", lambda e, o=psb[:, j * 128:(j + 1) * 128], i=ym[:, oc, mc * 128:(mc + 1) * 128]:
                         e.transpose(out=o, in_=i, identity=identb[:]),
                         reads=[r_ym[oc], r_identb], writes=[r_ps[bk]])
                copy_op(evac_eng(), vmm[:, mc, g8 * 1024:(g8 + 1) * 1024], psb[:, 0:1024],
                        reads=[r_ps[bk]], writes=[r_vmm])

        for ti in range(ntile):
            t0 = ti * T
            transpose_in(x, t0, 4, xT, r_xT, 0, key=f"t{ti}",
                         nxt=(x, t0 + T, 128, f"t{ti + 1}") if ti + 1 < ntile else None)

            if phases >= 1:
                rmsnorm(xT, r_xT, G_MIX, KC, T, hT, r_hT, 1.0 / D)

                def pool_group(g):
                    ab = g % 2
                    w = 2 ** (g + 1)
                    A = aext[ab]
                    P.op("dve", lambda e, o=A[:, :, 0:16], i=carry[:, 2 * g:2 * g + 2, :]: e.tensor_copy(out=o, in_=i),
                         reads=[r_carry[g]], writes=[r_aext[ab]])
                    P.op("dve", lambda e, o=carry[:, 2 * g:2 * g + 2, :], i=A[:, :, T:T + 16]: e.tensor_copy(out=o, in_=i),
                         reads=[r_aext[ab]], writes=[r_carry[g]])
                    src, r_srcs = A, [r_aext[ab]]
                    m = 1
                    step = 0
                    L = T + 16
                    while m < w:
                        dstb = ptmp[step % 2]
                        P.op("dve", lambda e, o=dstb[:, :, 2 * m - 1:L], i0=src[:, :, 2 * m - 1:L], i1=src[:, :, m - 1:L - m]:
                             e.tensor_tensor(out=o, in0=i0, in1=i1, op=ALU.add),
                             reads=r_srcs, writes=[r_ptmp[step % 2]])
                        src, r_srcs = dstb, [r_ptmp[step % 2]]
                        m *= 2
                        step += 1
                    pb = g % 2
                    P.op("dve", lambda e, o=pbf[pb][:, :, :], i0=src[:, :, 16:L], i1=A[:, :, 16:L]:
                         e.scalar_tensor_tensor(out=o, in0=i0, scalar=1.0 / w, in1=i1, op0=ALU.mult, op1=ALU.subtract),
                         reads=r_srcs + [r_aext[ab]], writes=[r_pbf[pb]])
                    if ti == 0:
                        other = ptmp[step % 2]
                        for cc in range(2):
                            P.op("dve", lambda e, o=other[:, cc, 0:16], i0=src[:, cc, 16:32], i1=invc[:, g * 16:(g + 1) * 16]:
                                 e.tensor_tensor(out=o, in0=i0, in1=i1, op=ALU.mult),
                                 reads=r_srcs + [r_invc], writes=[r_ptmp[step % 2]])
                            P.op("dve", lambda e, o=pbf[pb][:, cc, 0:16], i0=other[:, cc, 0:16], i1=A[:, cc, 16:32]:
                                 e.tensor_tensor(out=o, in0=i0, in1=i1, op=ALU.subtract),
                                 reads=[r_ptmp[step % 2], r_aext[ab]], writes=[r_pbf[pb]])
                def pool_mm(g):
                    pb = g % 2
                    for oc in range(2):
                        bk = next_bank()
                        for ic in range(2):
                            P.op("pe", lambda e, o=ps[:, bk, :], l=poolw[:, g, ic, oc * 128:(oc + 1) * 128],
                                 r=pbf[pb][:, ic, :], st=(ic == 0), sp_=(ic == 1):
                                 e.matmul(o, lhsT=l, rhs=r, start=st, stop=sp_),
                                 reads=[r_poolw, r_pbf[pb]], writes=[r_ps[bk]])
                        ch = 2 * g + oc
                        copy_op("act", ym[:, ch, :], ps[:, bk, :], reads=[r_ps[bk]], writes=[r_ym[ch]],
                                scale=gv[:, G_PSCALE + ch:G_PSCALE + ch + 1])

                def pool_evac(oc, bk):
                    g = oc // 2
                    copy_op("act", aext[g % 2][:, oc % 2, 16:T + 16], ps[:, bk, :],
                            reads=[r_ps[bk]], writes=[r_aext[g % 2]])
                    if oc % 2 == 1:
                        pool_group(g)
                        if g >= 1:
                            pool_mm(g - 1)

                def u_evac(oc, bk):
                    copy_op("act", ym[:, 8 + oc, :], ps[:, bk, :], reads=[r_ps[bk]], writes=[r_ym[8 + oc]])

                def v_evac(j, bk):
                    q = j % 4
                    if j == 3:
                        pool_mm(3)
                    if j > 0:
                        vstat_mm(j - 1)
                    P.op("act", lambda e, o=sq[q][:, :], i=ps[:, bk, :]: e.activation(out=o, in_=i, func=AF.Square),
                         reads=[r_ps[bk]], writes=[r_sq[q]])
                    copy_op("act", vfm[:, j, :], ps[:, bk, :], reads=[r_ps[bk]], writes=[r_vfm[j]])

                def vstat_mm(j):
                    q = j % 4
                    P.op("pe", lambda e, r=sq[q][:, :], st=(j == 0), sp_=(j == 7):
                         e.matmul(ps[:, STAT, :], lhsT=onesb[:], rhs=r, start=st, stop=sp_),
                         reads=[r_sq[q], r_ones], writes=[r_ps[STAT]])
                hrhs = lambda k: hT[:, k, :]
                hres = lambda k: r_hT[k]
                proj("w_in", 0, 8, KC, 0, hrhs, hres, T, pool_evac)
                proj("w_in", 2048, 8, KC, 0, hrhs, hres, T, v_evac)
                vstat_mm(7)

                if True:
                  P.op("act", lambda e: e.activation(out=lnv[:], in_=ps[:, STAT, :], func=AF.Ln, scale=1.0 / 1024,
                                                   bias=eps_ap[:, 0:1]),
                     reads=[r_ps[STAT], r_eps], writes=[r_lnv])
                  P.op("act", lambda e: e.activation(out=ps[:, STAT, :], in_=lnv[:], func=AF.Exp, scale=-0.5),
                     reads=[r_lnv], writes=[r_ps[STAT]])
                for j in range(8):
                    P.op("dve", lambda e, o=vfm[:, j, :], s=gv[:, G_SGU + j:G_SGU + j + 1]:
                         e.scalar_tensor_tensor(out=o, in0=o, scalar=s, in1=ps[:, STAT, :], op0=ALU.mult, op1=ALU.mult),
                         reads=[r_gv, r_ps[STAT]], writes=[r_vfm[j]])
                proj("w_in", 1024, 8, KC, 0, hrhs, hres, T, u_evac)
                for blk in range(4):
                    bk = next_bank()
                    psb = ps[:, bk, :].bitcast(BF16)
                    for j in range(8):
                        P.op("pe", lambda e, o=psb[:, j * 128:(j + 1) * 128], i=vfm[:, j, blk * 128:(blk + 1) * 128]:
                             e.transpose(out=o, in_=i, identity=identb[:]),
                             reads=[r_vfm[j], r_identb], writes=[r_ps[bk]])
                    copy_op(evac_eng(), vtm[:, blk, :], psb[:, 0:1024], reads=[r_ps[bk]], writes=[r_hid[2 * blk], r_hid[2 * blk + 1]])
                for h in range(8):
                    bk = next_bank()
                    for blk in range(4):
                        o = ps[:, bk, blk * 128:(blk + 1) * 128]
                        P.op("pe", lambda e, o=o, l=vtm[:, blk, h * 128:(h + 1) * 128], r=wsT[:, h, :]:
                             e.matmul(o, lhsT=l, rhs=r, start=True, stop=False),
                             reads=[r_hid[2 * blk], r_hid[2 * blk + 1], r_wsT], writes=[r_ps[bk]])
                        P.op("pe", lambda e, o=o, r=bmat[:, h * 128:(h + 1) * 128]:
                             e.matmul(o, lhsT=onesb[:], rhs=r, start=False, stop=True),
                             reads=[r_bmat, r_ones], writes=[r_ps[bk]])
                    P.op("dve", lambda e, o=ym[:, 8 + h, :], i0=ps[:, bk, :]:
                         e.tensor_tensor(out=o, in0=i0, in1=o, op=ALU.mult),
                         reads=[r_ps[bk]], writes=[r_ym[8 + h]])
                if True:
                    proj("w_out", 0, KC, KC, 0, lambda k: ym[:, k, :], lambda k: r_ym[k], T, resid_add)

            if phases >= 2:
                rmsnorm(xT, r_xT, G_XATTN, KC, T, hT, r_hT, 1.0 / D)

                def q_evac(oc, bk):
                    copy_op("act", ym[:, oc, :], ps[:, bk, :], reads=[r_ps[bk]], writes=[r_ym[oc]],
                            scale=float(512 ** -0.5))
                proj("w_q", 0, KC, KC, 0, lambda k: hT[:, k, :], lambda k: r_hT[k], T, q_evac)
                for h in range(4):
                    eb = h % 2
                    for mc in range(2):
                        bk = next_bank()
                        for dc in range(4):
                            ch = 4 * h + dc
                            P.op("pe", lambda e, o=ps[:, bk, :], l=kT[:, ch, mc * 128:(mc + 1) * 128], r=ym[:, ch, :],
                                 st=(dc == 0), sp_=(dc == 3): e.matmul(o, lhsT=l, rhs=r, start=st, stop=sp_),
                                 reads=[r_kT, r_ym[ch]], writes=[r_ps[bk]])
                        P.op("act", lambda e, o=expT[eb][:, mc, :], i=ps[:, bk, :]: e.activation(out=o, in_=i, func=AF.Exp),
                             reads=[r_ps[bk]], writes=[r_expT[eb]])
                    bk = next_bank()
                    for mc in range(2):
                        P.op("pe", lambda e, o=ps[:, bk, :], r=expT[eb][:, mc, :], st=(mc == 0), sp_=(mc == 1):
                             e.matmul(o, lhsT=onesb[:], rhs=r, start=st, stop=sp_),
                             reads=[r_expT[eb], r_ones], writes=[r_ps[bk]])
                    P.op("dve", lambda e, o=rden[eb][:], i=ps[:, bk, :]: e.reciprocal(out=o, in_=i),
                         reads=[r_ps[bk]], writes=[r_rden[eb]])
                    for dc in range(4):
                        ch = 4 * h + dc
                        bk = next_bank()
                        for mc in range(2):
                            P.op("pe", lambda e, o=ps[:, bk, :], l=vmm[:, mc, ch * 128:(ch + 1) * 128], r=expT[eb][:, mc, :],
                                 st=(mc == 0), sp_=(mc == 1): e.matmul(o, lhsT=l, rhs=r, start=st, stop=sp_),
                                 reads=[r_vmm, r_expT[eb]], writes=[r_ps[bk]])
                        P.op("dve", lambda e, o=hT[:, ch, :], i0=ps[:, bk, :], i1=rden[eb][:]:
                             e.tensor_tensor(out=o, in0=i0, in1=i1, op=ALU.mult),
                             reads=[r_ps[bk], r_rden[eb]], writes=[r_hT[ch]])
                proj("w_o", 0, KC, KC, 0, lambda k: hT[:, k, :], lambda k: r_hT[k], T, resid_add)

            if phases >= 3:
                rmsnorm(xT, r_xT, G_FFN, KC, T, hT, r_hT, 1.0 / D)
                for (f0, nf) in FF_GROUPS:
                    for pc in range(0, nf, PW // 128):
                        c0 = (f0 + pc) * 128
                        sg_ = next_panel("w_gate", 0, KC, c0)
                        for cc in range(PW // 128):
                            bg = next_bank()
                            for k in range(KC):
                                P.op("pe", lambda e, o=ps[:, bg, :], l=wsl[sg_][:, k, cc * 128:(cc + 1) * 128], r=hT[:, k, :],
                                     st=(k == 0), sp_=(k == KC - 1): e.matmul(o, lhsT=l, rhs=r, start=st, stop=sp_),
                                     reads=[r_wsl[sg_], r_hT[k]], writes=[r_ps[bg]])
                            P.op("act", lambda e, o=sgt[cc][:], i=ps[:, bg, :]: e.activation(out=o, in_=i, func=AF.Silu),
                                 reads=[r_ps[bg]], writes=[r_sgt[cc]])
                        su_ = next_panel("w_up", 0, KC, c0)
                        for cc in range(PW // 128):
                            j = pc + cc
                            bu = next_bank()
                            for k in range(KC):
                                P.op("pe", lambda e, o=ps[:, bu, :], l=wsl[su_][:, k, cc * 128:(cc + 1) * 128], r=hT[:, k, :],
                                     st=(k == 0), sp_=(k == KC - 1): e.matmul(o, lhsT=l, rhs=r, start=st, stop=sp_),
                                     reads=[r_wsl[su_], r_hT[k]], writes=[r_ps[bu]])
                            P.op("dve", lambda e, o=hid[:, j, :], i0=ps[:, bu, :], i1=sgt[cc][:]:
                                 e.tensor_tensor(out=o, in0=i0, in1=i1, op=ALU.mult),
                                 reads=[r_ps[bu], r_sgt[cc]], writes=[r_hid[j]])
                    proj("w_down", 0, KC, nf, f0 * 128, lambda k: hid[:, k, :], lambda k: r_hid[k], T, resid_add)

            for kc in range(KC):
                q = kc % 4
                P.op("act", lambda e, o=sq[q][:, :], i=xT[:, kc, :]: e.activation(out=o, in_=i, func=AF.Square),
                     reads=[r_xT[kc]], writes=[r_sq[q]])
                for blk in range(4):
                    P.op("pe", lambda e, o=ps[:, STAT, blk:blk + 1], l=sq[q][:, blk * 128:(blk + 1) * 128],
                         st=(kc == 0 and blk == 0), sp_=(kc == KC - 1):
                         e.matmul(o, lhsT=l, rhs=onesb[:, 0:1], start=st, stop=sp_, skip_group_check=True),
                         reads=[r_sq[q], r_ones], writes=[r_ps[STAT]])
            P.op("act", lambda e: e.activation(out=lnc[:], in_=ps[:, STAT, 0:4], func=AF.Ln, scale=1.0 / D,
                                               bias=eps_ap[:, 0:1]),
                 reads=[r_ps[STAT], r_eps], writes=[r_lnc])
            P.op("act", lambda e: e.activation(out=rstdc[:], in_=lnc[:], func=AF.Exp, scale=-0.5),
                 reads=[r_lnc], writes=[r_rstdc])
            cnt = 0
            for blk in range(4):
                for half in range(2):
                    ob = cnt % 2
                    cnt += 1
                    for cg in range(2):
                        bk = next_bank()
                        for j in range(4):
                            kc = half * 8 + cg * 4 + j
                            P.op("pe", lambda e, o=ps[:, bk, j * 128:(j + 1) * 128], i=xT[:, kc, blk * 128:(blk + 1) * 128]:
                                 e.transpose(out=o, in_=i, identity=identf[:]),
                                 reads=[r_xT[kc], r_identf], writes=[r_ps[bk]])
                        gc0 = half * 1024 + cg * 512
                        P.op("dve", lambda e, o=xout[ob][:, cg * 512:(cg + 1) * 512], i0=ps[:, bk, :],
                             s_=rstdc[:, blk:blk + 1], i1=gbc[:, gc0:gc0 + 512]:
                             e.scalar_tensor_tensor(out=o, in0=i0, scalar=s_, in1=i1, op0=ALU.mult, op1=ALU.mult),
                             reads=[r_ps[bk], r_rstdc, r_gbc], writes=[r_xout[ob]])
                    P.op("sp", lambda e, i=xout[ob][:], o=y[t0 + blk * 128:t0 + (blk + 1) * 128,
                                                           half * 1024:(half + 1) * 1024]:
                         e.dma_start(out=o, in_=i), reads=[r_xout[ob]], dma=f"st{ob}")

        if record is not None:
            return None
        assert wpos["next"] == len(sched), (wpos, len(sched))
        P.op("sp", lambda e: None, writes=[r_xout[0], r_xout[1]])

        P.resolve()
        sems = {e: es.enter_context(nc.semaphore(f"s_{e}")) for e in ("pe", "act", "dve")}
        dsem = {k: es.enter_context(nc.semaphore(f"d_{k}")) for k in P.dma_cnt}

        def body(engname):
            def run(e):
                rank = P.rank.get(engname, {})
                for idx, ins in enumerate(P.ins[engname]):
                    for (kind, key, val) in ins["waits"]:
                        if kind == "eng":
                            e.wait_ge(sems[key], P.rank[key][val])
                        else:
                            e.wait_ge(dsem[key], val)
                    r = ins["fn"](e)
                    if ins["dma"] is not None:
                        r.then_inc(dsem[ins["dma"]], 16)
                    elif idx in rank:
                        r.then_inc(sems[engname], 1)
            return run

        with nc.Block() as block:
            block.sync(body("sp"))
            block.gpsimd(body("pool"))
            block.tensor(body("pe"))
            block.scalar(body("act"))
            block.vector(body("dve"))
    return nc


_NC_CACHE = {}


def _get_nc():
    if "nc" not in _NC_CACHE:
        _NC_CACHE["nc"] = _build()
    return _NC_CACHE["nc"]


def kernel(x, mem, norm_mix_g, w_in, pool_w, pool_scale, sgu_norm_g, w_spatial, b_spatial, w_out,
           norm_xattn_g, norm_mem_g, w_q, w_k, w_v, w_o, norm_ffn_g, w_gate, w_up, w_down, final_norm_g):
    f = lambda a: np.ascontiguousarray(np.asarray(a, dtype=np.float32))
    x = f(x)
    mem = f(mem)
    B, S, _ = x.shape

    def fm(vec):
        v = f(vec).reshape(-1)
        return v.reshape(-1, 128).T

    gvh = np.ascontiguousarray(np.concatenate(
        [fm(norm_mix_g[0]), fm(norm_xattn_g[0]), fm(norm_mem_g[0]), fm(norm_ffn_g[0]), fm(final_norm_g),
         fm(pool_scale[0]), fm(sgu_norm_g[0])], axis=1).astype(np.float32))
    assert gvh.shape == (128, NG)
    wsT_h = np.ascontiguousarray(np.transpose(f(w_spatial)[0], (2, 0, 1)).reshape(128, 8 * 128))
    brow_h = f(b_spatial)[0].reshape(1, 1024)
    ident_h = np.eye(128, dtype=np.float32)
    shared = dict(
        w_in=f(w_in)[0], pool_w=f(pool_w)[0], w_out=f(w_out)[0], w_q=f(w_q)[0], w_k=f(w_k)[0], w_v=f(w_v)[0],
        w_o=f(w_o)[0], w_gate=f(w_gate)[0], w_up=f(w_up)[0], w_down=f(w_down)[0],
        gv=gvh, wsT=wsT_h, brow=brow_h, ident=ident_h,
        gbc=np.ascontiguousarray(np.broadcast_to(f(final_norm_g).reshape(1, D), (128, D))))
    in_maps = []
    for c in range(8):
        b, half = c // 2, c % 2
        s0 = half * TOK
        xc = x[b, s0:s0 + TOK]
        if half == 0:
            xhc = np.zeros((16, D), np.float32)
        else:
            xhc = x[b, s0 - 16:s0]
        pos = np.arange(s0 + 1, s0 + 17, dtype=np.float32)
        tab = np.concatenate([1.0 / np.minimum(pos, float(w)) for w in (2, 4, 8, 16)]).astype(np.float32)
        m = dict(shared)
        m.update(x=np.ascontiguousarray(xc), xh=np.ascontiguousarray(xhc), mem=np.ascontiguousarray(mem[b]),
                 invc=np.ascontiguousarray(np.broadcast_to(tab[None, :], (128, 64))))
        in_maps.append(m)
    nc = _get_nc()
    res = run_bass_kernel_spmd(nc, in_maps, core_ids=list(range(8)))
    out = np.empty((B, S, D), np.float32)
    for c in range(8):
        b, half = c // 2, c % 2
        out[b, half * TOK:(half + 1) * TOK] = res.results[c]["y"]
    return out
```

```python
import numpy as np
from contextlib import ExitStack

import concourse.bass as bass
import concourse.mybir as mybir
from concourse.bass_utils import run_bass_kernel_spmd

F32 = mybir.dt.float32
BF16 = mybir.dt.bfloat16
AF = mybir.ActivationFunctionType
ALU = mybir.AluOpType

D = 2048
KC = 16
T = 512
NTILE = 4
TOK = 2048
NMEM = 256
DFF = 5632
EPS = 1e-6
NSLOT = 4
PW = 256
FF_GROUPS = [(0, 12), (12, 12), (24, 12), (36, 8)]

G_MIX, G_XATTN, G_MEM, G_FFN, G_FINAL, G_PSCALE, G_SGU = 0, 16, 32, 48, 64, 80, 88
NG = 96


class Res:
    __slots__ = ("name", "w", "r")

    def __init__(self, name):
        self.name = name
        self.w = None
        self.r = {}


class Prog:
    ENGS = ("pe", "act", "dve", "pool", "sp")

    def __init__(self):
        self.ins = {e: [] for e in self.ENGS}
        self.dma_cnt = {}

    def op(self, eng, fn, reads=(), writes=(), dma=None):
        deps = []
        for t in reads:
            if t.w is not None:
                deps.append(t.w)
        for t in writes:
            if t.w is not None:
                deps.append(t.w)
            deps.extend(t.r.values())
        idx = len(self.ins[eng])
        if dma is not None:
            self.dma_cnt[dma] = self.dma_cnt.get(dma, 0) + 16
            ev = ("dma", dma, self.dma_cnt[dma])
        else:
            ev = ("eng", eng, idx)
        self.ins[eng].append(dict(fn=fn, deps=deps, ev=ev, dma=dma))
        for t in reads:
            t.r[(ev[0], ev[1])] = ev
        for t in writes:
            t.w = ev
            t.r = {}
        return ev

    def resolve(self):
        need = {e: set() for e in self.ENGS}
        for e in self.ENGS:
            seen = {}
            for ins in self.ins[e]:
                best = {}
                for d in ins["deps"]:
                    kind, key, val = d
                    if kind == "eng" and key == e and e == "pe":
                        continue
                    k = (kind, key)
                    if val > best.get(k, -1):
                        best[k] = val
                waits = []
                for k, val in best.items():
                    if val <= seen.get(k, -1):
                        continue
                    seen[k] = val
                    waits.append((k[0], k[1], val))
                    if k[0] == "eng":
                        need[k[1]].add(val)
                ins["waits"] = waits
        self.rank = {}
        for e in self.ENGS:
            self.rank[e] = {idx: i + 1 for i, idx in enumerate(sorted(need[e]))}
        self.need = need


def _build(ntile=NTILE, phases=3, record=None):
    if record is None:
        rec = []
        _build(ntile, phases, record=rec)
        sched_in = rec
    else:
        sched_in = None
    nc = bass.Bass("TRN2", target_bir_lowering=False)
    dr = {}

    def din(name, shape):
        dr[name] = nc.dram_tensor(name, list(shape), F32, kind="ExternalInput").ap()
        return dr[name]

    x = din("x", [TOK, D])
    xh = din("xh", [16, D])
    mem = din("mem", [NMEM, D])
    w_in = din("w_in", [D, 3072])
    pool_w = din("pool_w", [4, 256, 256])
    w_out = din("w_out", [D, D])
    w_q = din("w_q", [D, D])
    w_k = din("w_k", [D, D])
    w_v = din("w_v", [D, D])
    w_o = din("w_o", [D, D])
    w_gate = din("w_gate", [D, DFF])
    w_up = din("w_up", [D, DFF])
    w_down = din("w_down", [DFF, D])
    gv_d = din("gv", [128, NG])
    invc_d = din("invc", [128, 64])
    wsT_d = din("wsT", [128, 8 * 128])
    brow_d = din("brow", [1, 1024])
    ident_d = din("ident", [128, 128])
    gbc_d = din("gbc", [128, D])
    y = nc.dram_tensor("y", [TOK, D], F32, kind="ExternalOutput").ap()
    W = dict(w_in=w_in, w_out=w_out, w_q=w_q, w_k=w_k, w_v=w_v, w_o=w_o,
             w_gate=w_gate, w_up=w_up, w_down=w_down)

    P = Prog()
    es = ExitStack()
    with es:
        def sb(name, shape, dt):
            return es.enter_context(nc.sbuf_tensor(name, list(shape), dt))

        xT = sb("xT", [128, KC, T], F32)
        hT = sb("hT", [128, KC, T], BF16)
        ym = sb("ym", [128, KC, T], BF16)
        vfm = sb("vfm", [128, 8, T], BF16)
        xin = [sb(f"xin{i}", [128, 512], F32) for i in range(4)]
        xout = [sb(f"xout{i}", [128, 512], F32) for i in range(4)]
        sq = [sb(f"sq{i}", [128, T], BF16) for i in range(4)]
        aext = [sb(f"aext{i}", [128, 2, T + 16], F32) for i in range(2)]
        ptmp = [sb(f"ptmp{i}", [128, 2, T + 16], F32) for i in range(2)]
        pbf = [sb(f"pbf{i}", [128, 2, T], BF16) for i in range(2)]
        carry = sb("carry", [128, 8, 16], F32)
        attnbuf = sb("attnbuf", [128, 4096], BF16)
        expT = [attnbuf[:, i * 1024:(i + 1) * 1024].rearrange("p (a t) -> p a t", a=2) for i in range(2)]
        rden = [attnbuf[:, 2048 + i * 1024:2048 + (i + 1) * 1024].bitcast(F32) for i in range(2)]
        hmT = attnbuf[:, :].rearrange("p (c m) -> p c m", c=KC)
        vtmp = [sb(f"vtmp{i}", [128, NMEM], BF16) for i in range(2)]
        sgt = [sb(f"sgt{i}", [128, T], F32) for i in range(2)]
        hid = sb("hid", [128, 12, T], BF16)
        vtm = hid[:, 0:8, :].rearrange("p (b two) t -> p b (two t)", two=2)
        lnv = sb("lnv", [128, T], F32)
        wsl = [sb(f"wsl{i}", [128, KC, PW], BF16) for i in range(NSLOT)]
        kT = sb("kT", [128, KC, NMEM], BF16)
        vmm = sb("vmm", [128, 2, D], BF16)
        gv = sb("gv_sb", [128, NG], F32)
        invc = sb("invc_sb", [128, 64], F32)
        wsT = sb("wsT_sb", [128, 8, 128], BF16)
        bmat = sb("bmat", [128, 1024], BF16)
        identf = sb("identf", [128, 128], F32)
        identb = sb("identb", [128, 128], BF16)
        onesb = sb("onesb", [128, 128], BF16)
        poolw = sb("poolw", [128, 4, 2, 256], BF16)
        xhT = sb("xhT", [128, KC, 16], F32)
        gbc = sb("gbc_sb", [128, D], F32)
        lnc = sb("lnc", [128, 4], F32)
        rstdc = sb("rstdc", [128, 4], F32)
        hhT = sb("hhT", [128, KC, 16], BF16)
        ps = es.enter_context(nc.psum_tensor("ps", [128, 8, 512], F32))

        R = lambda n: Res(n)
        r_xT = [R(f"xT{i}") for i in range(KC)]
        r_hT = [R(f"hT{i}") for i in range(KC)]
        r_ym = [R(f"ym{i}") for i in range(KC)]
        r_vfm = [R(f"vfm{i}") for i in range(8)]
        r_xin = [R(f"xin{i}") for i in range(4)]
        r_xout = [R(f"xout{i}") for i in range(4)]
        r_hmT = R("hmT")
        r_vtmp = [R("vtmp0"), R("vtmp1")]
        r_sq = [R(f"sq{i}") for i in range(4)]
        r_aext = [R("aext0"), R("aext1")]
        r_ptmp = [R("ptmp0"), R("ptmp1")]
        r_pbf = [R("pbf0"), R("pbf1")]
        r_carry = [R(f"carry{i}") for i in range(4)]
        r_expT = [R("expT0"), R("expT1")]
        r_rden = [R("rden0"), R("rden1")]
        r_sgt = [R("sgt0"), R("sgt1")]
        r_hid = [R(f"hid{i}") for i in range(12)]
        r_lnv = R("lnv")
        r_wsl = [R(f"wsl{i}") for i in range(NSLOT)]
        r_kT = R("kT")
        r_vmm = R("vmm")
        r_gv = R("gv")
        r_invc = R("invc")
        r_wsT = R("wsT")
        r_bmat = R("bmat")
        r_identf = R("identf")
        r_identb = R("identb")
        r_ones = R("ones")
        r_poolw = R("poolw")
        r_xhT = R("xhT")
        r_gbc = R("gbc")
        r_lnc = R("lnc")
        r_rstdc = R("rstdc")
        r_hhT = R("hhT")
        r_ps = [R(f"ps{i}") for i in range(8)]
        STAT = 7
        bank_ctr = [0]

        def next_bank():
            b = bank_ctr[0] % 7
            bank_ctr[0] += 1
            return b

        evac_ctr = [0]

        def evac_eng():
            evac_ctr[0] += 1
            return "act" if evac_ctr[0] % 2 else "dve"

        def copy_op(eng, out, in_, reads, writes, scale=None):
            if eng == "act":
                if scale is None:
                    P.op("act", lambda e, o=out, i=in_: e.activation(out=o, in_=i, func=AF.Copy),
                         reads=reads, writes=writes)
                else:
                    P.op("act", lambda e, o=out, i=in_, s=scale: e.activation(out=o, in_=i, func=AF.Copy, scale=s),
                         reads=reads, writes=writes)
            else:
                assert scale is None
                P.op("dve", lambda e, o=out, i=in_: e.tensor_copy(out=o, in_=i), reads=reads, writes=writes)

        P.op("sp", lambda e: e.dma_start(out=gv[:], in_=gv_d[:, :]), writes=[r_gv], dma="c_gv")
        P.op("sp", lambda e: e.dma_start(out=invc[:], in_=invc_d[:, :]), writes=[r_invc], dma="c_invc")
        P.op("sp", lambda e: e.dma_start(out=identf[:], in_=ident_d[:, :]), writes=[r_identf], dma="c_identf")
        P.op("sp", lambda e: e.dma_start(out=gbc[:], in_=gbc_d[:, :]), writes=[r_gbc], dma="c_gbc")
        P.op("pool", lambda e: e.dma_start(out=identb[:], in_=ident_d[:, :]), writes=[r_identb], dma="c_identb")
        P.op("pool", lambda e: e.dma_start(out=wsT[:], in_=wsT_d.rearrange("p (h t) -> p h t", h=8)),
             writes=[r_wsT], dma="c_wsT")
        P.op("pool", lambda e: e.dma_start(out=poolw[:], in_=pool_w.rearrange("g (ic p) d -> p g ic d", p=128)),
             writes=[r_poolw], dma="c_poolw")
        P.op("dve", lambda e: e.memset(onesb[:], 1.0), writes=[r_ones])
        P.op("dve", lambda e: e.memset(bmat[:], 0.0), writes=[r_bmat])
        P.op("pool", lambda e: e.dma_start(out=bmat[0:1, :], in_=brow_d[:, :]), writes=[r_bmat], dma="c_brow")
        P.op("dve", lambda e: e.memset(wsT[64:128, :, 0:64], 0.0), writes=[r_wsT])

        sched = sched_in if sched_in is not None else []
        wpos = {"issued": 0, "next": 0}

        def issue_loads(upto):
            while wpos["issued"] < min(upto, len(sched)):
                k = wpos["issued"]
                name, row0, nk, c0, ncols = sched[k]
                s = k % NSLOT
                src = W[name][row0:row0 + nk * 128, c0:c0 + ncols].rearrange("(kc p) n -> p kc n", p=128)
                dst = wsl[s][:, 0:nk, 0:ncols]
                P.op("pool", lambda e, d=dst, sr=src: e.dma_start(out=d, in_=sr),
                     writes=[r_wsl[s]], dma=f"w{s}")
                wpos["issued"] += 1

        def next_panel(name, row0, nk, c0):
            j = wpos["next"]
            if record is not None:
                record.append((name, row0, nk, c0, PW))
                wpos["next"] += 1
                return j % NSLOT
            assert sched[j][:4] == (name, row0, nk, c0), (sched[j], name, row0, nk, c0)
            issue_loads(j + NSLOT)
            wpos["next"] += 1
            return j % NSLOT

        preissued = {}

        def xload(src, row0, i, npart, nblk):
            cq, blk = divmod(i, nblk)
            b = i % 4
            P.op("sp", lambda e, o=xin[b][0:npart, :], i_=src[row0 + blk * 128: row0 + blk * 128 + npart,
                                                              cq * 512:(cq + 1) * 512]:
                 e.dma_start(out=o, in_=i_), writes=[r_xin[b]], dma=f"xin{b}")

        def transpose_in(src, row0, nblk, dstT, r_dst, colbase, npart=128, key=None, nxt=None):
            n = nblk * 4
            pre = preissued.pop(key, 0) if key is not None else 0
            for i in range(pre, min(4, n)):
                xload(src, row0, i, npart, nblk)
            for i in range(n):
                cq, blk = divmod(i, nblk)
                b = i % 4
                bk = next_bank()
                for j in range(4):
                    P.op("pe", lambda e, o=ps[:, bk, j * npart:(j + 1) * npart],
                         i_=xin[b][0:npart, j * 128:(j + 1) * 128], idn=identf[0:npart, 0:npart]:
                         e.transpose(out=o, in_=i_, identity=idn),
                         reads=[r_xin[b], r_identf], writes=[r_ps[bk]])
                kc0 = cq * 4
                c0 = colbase + blk * 128
                copy_op(evac_eng(), dstT[:, kc0:kc0 + 4, c0:c0 + npart],
                        ps[:, bk, 0:4 * npart].rearrange("p (j t) -> p j t", j=4),
                        reads=[r_ps[bk]], writes=r_dst[kc0:kc0 + 4] if isinstance(r_dst, list) else [r_dst])
                if i + 4 < n:
                    xload(src, row0, i + 4, npart, nblk)
                elif nxt is not None:
                    nsrc, nrow0, nnpart, nkey, nnblk = nxt
                    xload(nsrc, nrow0, i + 4 - n, nnpart, nnblk)
                    preissued[nkey] = preissued.get(nkey, 0) + 1

        def rmsnorm(srcT, r_src, goff, nch, ncols, dst, r_dst, inv_n, dst_f32=False):
            rs = lambda c: r_src[c] if isinstance(r_src, list) else r_src
            rd = lambda c: r_dst[c] if isinstance(r_dst, list) else r_dst
            for c in range(nch):
                q = c % 4
                P.op("act", lambda e, o=sq[q][:, 0:ncols], i=srcT[:, c, 0:ncols]:
                     e.activation(out=o, in_=i, func=AF.Square), reads=[rs(c)], writes=[r_sq[q]])
                P.op("pe", lambda e, o=ps[:, STAT, 0:ncols], r=sq[q][:, 0:ncols], st=(c == 0), sp_=(c == nch - 1):
                     e.matmul(o, lhsT=onesb[:], rhs=r, start=st, stop=sp_),
                     reads=[r_sq[q], r_ones], writes=[r_ps[STAT]])
            P.op("act", lambda e, o=lnv[:, 0:ncols], i=ps[:, STAT, 0:ncols]:
                 e.activation(out=o, in_=i, func=AF.Ln, scale=inv_n, bias=eps_ap[:, 0:1]),
                 reads=[r_ps[STAT], r_eps], writes=[r_lnv])
            P.op("act", lambda e, o=ps[:, STAT, 0:ncols], i=lnv[:, 0:ncols]:
                 e.activation(out=o, in_=i, func=AF.Exp, scale=-0.5),
                 reads=[r_lnv], writes=[r_ps[STAT]])
            for c in range(nch):
                P.op("dve", lambda e, o=dst[:, c, 0:ncols], i0=srcT[:, c, 0:ncols], s=gv[:, goff + c:goff + c + 1],
                     i1=ps[:, STAT, 0:ncols]:
                     e.scalar_tensor_tensor(out=o, in0=i0, scalar=s, in1=i1, op0=ALU.mult, op1=ALU.mult),
                     reads=[rs(c), r_gv, r_ps[STAT]], writes=[rd(c)])

        def proj(name, col0, nchunks, nk, row0, rhs_fn, r_rhs_fn, ncols, evac_fn, extra_fn=None, after_panel=None):
            for pc in range(0, nchunks, PW // 128):
                s = next_panel(name, row0, nk, col0 + pc * 128)
                for cc in range(PW // 128):
                    oc = pc + cc
                    bk = next_bank()
                    for k in range(nk):
                        P.op("pe", lambda e, o=ps[:, bk, 0:ncols], l=wsl[s][:, k, cc * 128:(cc + 1) * 128],
                             r=rhs_fn(k), st=(k == 0), sp_=(k == nk - 1):
                             e.matmul(o, lhsT=l, rhs=r, start=st, stop=sp_),
                             reads=[r_wsl[s], r_rhs_fn(k)], writes=[r_ps[bk]])
                    if extra_fn is not None:
                        extra_fn(s, cc, oc)
                    evac_fn(oc, bk)
                if after_panel is not None:
                    after_panel()

        def resid_add(oc, bk):
            P.op("dve", lambda e, o=xT[:, oc, :], i0=ps[:, bk, :], i1=xT[:, oc, :]:
                 e.tensor_tensor(out=o, in0=i0, in1=i1, op=ALU.add),
                 reads=[r_ps[bk]], writes=[r_xT[oc]])

        eps_ap = sb("eps_sb", [128, 1], F32)
        r_eps = R("eps")
        P.op("dve", lambda e: e.memset(eps_ap[:], EPS), writes=[r_eps])

        transpose_in(xh, 0, 1, xhT, r_xhT, 0, npart=16, key="halo", nxt=(mem, 0, 128, "mem", 2))
        rmsnorm(xhT, r_xhT, G_MIX, KC, 16, hhT, r_hhT, 1.0 / D)

        def halo_extra(s, cc, oc):
            bk2 = next_bank()
            for k in range(KC):
                P.op("pe", lambda e, o=ps[:, bk2, 0:16], l=wsl[s][:, k, cc * 128:(cc + 1) * 128], r=hhT[:, k, :],
                     st=(k == 0), sp_=(k == KC - 1): e.matmul(o, lhsT=l, rhs=r, start=st, stop=sp_),
                     reads=[r_wsl[s], r_hhT], writes=[r_ps[bk2]])
            copy_op("act", carry[:, oc, :], ps[:, bk2, 0:16], reads=[r_ps[bk2]], writes=[r_carry[oc // 2]])

        transpose_in(mem, 0, 2, xT, r_xT, 0, key="mem", nxt=(x, 0, 128, "t0", 4))
        rmsnorm(xT, r_xT, G_MEM, KC, NMEM, hmT, r_hmT, 1.0 / D)
        kv_todo = [("w_k", pc) for pc in range(0, KC, 2)] + [("w_v", pc) for pc in range(0, KC, 2)]
        kv_pending = []

        def kv_flush():
            while kv_pending:
                oc = kv_pending.pop(0)
                bk2 = next_bank()
                psb = ps[:, bk2, :].bitcast(BF16)
                for mc in range(2):
                    P.op("pe", lambda e, o=psb[:, mc * 128:(mc + 1) * 128], i=vtmp[oc % 2][:, mc * 128:(mc + 1) * 128]:
                         e.transpose(out=o, in_=i, identity=identb[:]),
                         reads=[r_vtmp[oc % 2], r_identb], writes=[r_ps[bk2]])
                copy_op(evac_eng(), vmm[:, 0:2, oc * 128:(oc + 1) * 128],
                        psb[:, 0:256].rearrange("p (a t) -> p a t", a=2), reads=[r_ps[bk2]], writes=[r_vmm])

        def kv_step():
            if not kv_todo:
                kv_flush()
                return
            name, pc = kv_todo.pop(0)
            s = next_panel(name, 0, KC, pc * 128)
            for cc in range(2):
                oc = pc + cc
                bk = next_bank()
                for k in range(KC):
                    P.op("pe", lambda e, o=ps[:, bk, 0:NMEM], l=wsl[s][:, k, cc * 128:(cc + 1) * 128], r=hmT[:, k, :],
                         st=(k == 0), sp_=(k == KC - 1): e.matmul(o, lhsT=l, rhs=r, start=st, stop=sp_),
                         reads=[r_wsl[s], r_hmT], writes=[r_ps[bk]])
                if name == "w_k":
                    copy_op(evac_eng(), kT[:, oc, :], ps[:, bk, 0:NMEM], reads=[r_ps[bk]], writes=[r_kT])
                else:
                    if cc == 0:
                        kv_flush()
                    copy_op("act", vtmp[oc % 2][:], ps[:, bk, 0:NMEM], reads=[r_ps[bk]], writes=[r_vtmp[oc % 2]])
                    kv_pending.append(oc)

        for ti in range(ntile):
            t0 = ti * T
            transpose_in(x, t0, 4, xT, r_xT, 0, key=f"t{ti}",
                         nxt=(x, t0 + T, 128, f"t{ti + 1}", 4) if ti + 1 < ntile else None)
            kvh = kv_step if ti == 0 else None

            if phases >= 1:
                rmsnorm(xT, r_xT, G_MIX, KC, T, hT, r_hT, 1.0 / D)

                def pool_group(g):
                    ab = g % 2
                    w = 2 ** (g + 1)
                    A = aext[ab]
                    P.op("dve", lambda e, o=A[:, :, 0:16], i=carry[:, 2 * g:2 * g + 2, :]: e.tensor_copy(out=o, in_=i),
                         reads=[r_carry[g]], writes=[r_aext[ab]])
                    P.op("dve", lambda e, o=carry[:, 2 * g:2 * g + 2, :], i=A[:, :, T:T + 16]: e.tensor_copy(out=o, in_=i),
                         reads=[r_aext[ab]], writes=[r_carry[g]])
                    src, r_srcs = A, [r_aext[ab]]
                    m = 1
                    step = 0
                    L = T + 16
                    while m < w:
                        dstb = ptmp[step % 2]
                        P.op("dve", lambda e, o=dstb[:, :, 2 * m - 1:L], i0=src[:, :, 2 * m - 1:L], i1=src[:, :, m - 1:L - m]:
                             e.tensor_tensor(out=o, in0=i0, in1=i1, op=ALU.add),
                             reads=r_srcs, writes=[r_ptmp[step % 2]])
                        src, r_srcs = dstb, [r_ptmp[step % 2]]
                        m *= 2
                        step += 1
                    pb = g % 2
                    P.op("dve", lambda e, o=pbf[pb][:, :, :], i0=src[:, :, 16:L], i1=A[:, :, 16:L]:
                         e.scalar_tensor_tensor(out=o, in0=i0, scalar=1.0 / w, in1=i1, op0=ALU.mult, op1=ALU.subtract),
                         reads=r_srcs + [r_aext[ab]], writes=[r_pbf[pb]])
                    if ti == 0:
                        other = ptmp[step % 2]
                        for cc in range(2):
                            P.op("dve", lambda e, o=other[:, cc, 0:16], i0=src[:, cc, 16:32], i1=invc[:, g * 16:(g + 1) * 16]:
                                 e.tensor_tensor(out=o, in0=i0, in1=i1, op=ALU.mult),
                                 reads=r_srcs + [r_invc], writes=[r_ptmp[step % 2]])
                            P.op("dve", lambda e, o=pbf[pb][:, cc, 0:16], i0=other[:, cc, 0:16], i1=A[:, cc, 16:32]:
                                 e.tensor_tensor(out=o, in0=i0, in1=i1, op=ALU.subtract),
                                 reads=[r_ptmp[step % 2], r_aext[ab]], writes=[r_pbf[pb]])
                def pool_mm(g):
                    pb = g % 2
                    for oc in range(2):
                        bk = next_bank()
                        for ic in range(2):
                            P.op("pe", lambda e, o=ps[:, bk, :], l=poolw[:, g, ic, oc * 128:(oc + 1) * 128],
                                 r=pbf[pb][:, ic, :], st=(ic == 0), sp_=(ic == 1):
                                 e.matmul(o, lhsT=l, rhs=r, start=st, stop=sp_),
                                 reads=[r_poolw, r_pbf[pb]], writes=[r_ps[bk]])
                        ch = 2 * g + oc
                        copy_op("act", ym[:, ch, :], ps[:, bk, :], reads=[r_ps[bk]], writes=[r_ym[ch]],
                                scale=gv[:, G_PSCALE + ch:G_PSCALE + ch + 1])

                def pool_evac(oc, bk):
                    g = oc // 2
                    copy_op("act", aext[g % 2][:, oc % 2, 16:T + 16], ps[:, bk, :],
                            reads=[r_ps[bk]], writes=[r_aext[g % 2]])
                    if oc % 2 == 1:
                        pool_group(g)
                        if g >= 1:
                            pool_mm(g - 1)

                def u_evac(oc, bk):
                    copy_op("act", ym[:, 8 + oc, :], ps[:, bk, :], reads=[r_ps[bk]], writes=[r_ym[8 + oc]])

                def v_evac(j, bk):
                    q = j % 4
                    if j == 3:
                        pool_mm(3)
                    if j > 0:
                        vstat_mm(j - 1)
                    P.op("act", lambda e, o=sq[q][:, :], i=ps[:, bk, :]: e.activation(out=o, in_=i, func=AF.Square),
                         reads=[r_ps[bk]], writes=[r_sq[q]])
                    copy_op("act", vfm[:, j, :], ps[:, bk, :], reads=[r_ps[bk]], writes=[r_vfm[j]])

                def vstat_mm(j):
                    q = j % 4
                    P.op("pe", lambda e, r=sq[q][:, :], st=(j == 0), sp_=(j == 7):
                         e.matmul(ps[:, STAT, :], lhsT=onesb[:], rhs=r, start=st, stop=sp_),
                         reads=[r_sq[q], r_ones], writes=[r_ps[STAT]])
                hrhs = lambda k: hT[:, k, :]
                hres = lambda k: r_hT[k]
                proj("w_in", 0, 8, KC, 0, hrhs, hres, T, pool_evac,
                     extra_fn=halo_extra if ti == 0 else None, after_panel=kvh)
                proj("w_in", 2048, 8, KC, 0, hrhs, hres, T, v_evac, after_panel=kvh)
                vstat_mm(7)

                if True:
                  P.op("act", lambda e: e.activation(out=lnv[:], in_=ps[:, STAT, :], func=AF.Ln, scale=1.0 / 1024,
                                                   bias=eps_ap[:, 0:1]),
                     reads=[r_ps[STAT], r_eps], writes=[r_lnv])
                  P.op("act", lambda e: e.activation(out=ps[:, STAT, :], in_=lnv[:], func=AF.Exp, scale=-0.5),
                     reads=[r_lnv], writes=[r_ps[STAT]])
                for j in range(8):
                    P.op("dve", lambda e, o=vfm[:, j, :], s=gv[:, G_SGU + j:G_SGU + j + 1]:
                         e.scalar_tensor_tensor(out=o, in0=o, scalar=s, in1=ps[:, STAT, :], op0=ALU.mult, op1=ALU.mult),
                         reads=[r_gv, r_ps[STAT]], writes=[r_vfm[j]])
                proj("w_in", 1024, 8, KC, 0, hrhs, hres, T, u_evac, after_panel=kvh)
                for blk in range(4):
                    bk = next_bank()
                    psb = ps[:, bk, :].bitcast(BF16)
                    for j in range(8):
                        P.op("pe", lambda e, o=psb[:, j * 128:(j + 1) * 128], i=vfm[:, j, blk * 128:(blk + 1) * 128]:
                             e.transpose(out=o, in_=i, identity=identb[:]),
                             reads=[r_vfm[j], r_identb], writes=[r_ps[bk]])
                    copy_op(evac_eng(), vtm[:, blk, :], psb[:, 0:1024], reads=[r_ps[bk]], writes=[r_hid[2 * blk], r_hid[2 * blk + 1]])
                for h in range(8):
                    bk = next_bank()
                    for blk in range(4):
                        o = ps[:, bk, blk * 128:(blk + 1) * 128]
                        P.op("pe", lambda e, o=o, l=vtm[:, blk, h * 128:(h + 1) * 128], r=wsT[:, h, :]:
                             e.matmul(o, lhsT=l, rhs=r, start=True, stop=False),
                             reads=[r_hid[2 * blk], r_hid[2 * blk + 1], r_wsT], writes=[r_ps[bk]])
                        P.op("pe", lambda e, o=o, r=bmat[:, h * 128:(h + 1) * 128]:
                             e.matmul(o, lhsT=onesb[:], rhs=r, start=False, stop=True),
                             reads=[r_bmat, r_ones], writes=[r_ps[bk]])
                    P.op("dve", lambda e, o=ym[:, 8 + h, :], i0=ps[:, bk, :]:
                         e.tensor_tensor(out=o, in0=i0, in1=o, op=ALU.mult),
                         reads=[r_ps[bk]], writes=[r_ym[8 + h]])
                proj("w_out", 0, KC, KC, 0, lambda k: ym[:, k, :], lambda k: r_ym[k], T, resid_add, after_panel=kvh)
                if ti == 0:
                    assert not kv_todo
                    kv_flush()

            if phases >= 2:
                rmsnorm(xT, r_xT, G_XATTN, KC, T, hT, r_hT, 1.0 / D)

                def q_evac(oc, bk):
                    copy_op("act", ym[:, oc, :], ps[:, bk, :], reads=[r_ps[bk]], writes=[r_ym[oc]],
                            scale=float(512 ** -0.5))
                proj("w_q", 0, KC, KC, 0, lambda k: hT[:, k, :], lambda k: r_hT[k], T, q_evac)
                for h in range(4):
                    eb = h % 2
                    for mc in range(2):
                        bk = next_bank()
                        for dc in range(4):
                            ch = 4 * h + dc
                            P.op("pe", lambda e, o=ps[:, bk, :], l=kT[:, ch, mc * 128:(mc + 1) * 128], r=ym[:, ch, :],
                                 st=(dc == 0), sp_=(dc == 3): e.matmul(o, lhsT=l, rhs=r, start=st, stop=sp_),
                                 reads=[r_kT, r_ym[ch]], writes=[r_ps[bk]])
                        P.op("act", lambda e, o=expT[eb][:, mc, :], i=ps[:, bk, :]: e.activation(out=o, in_=i, func=AF.Exp),
                             reads=[r_ps[bk]], writes=[r_expT[eb]] + ([r_hmT] if ti == 0 else []))
                    bk = next_bank()
                    for mc in range(2):
                        P.op("pe", lambda e, o=ps[:, bk, :], r=expT[eb][:, mc, :], st=(mc == 0), sp_=(mc == 1):
                             e.matmul(o, lhsT=onesb[:], rhs=r, start=st, stop=sp_),
                             reads=[r_expT[eb], r_ones], writes=[r_ps[bk]])
                    P.op("dve", lambda e, o=rden[eb][:], i=ps[:, bk, :]: e.reciprocal(out=o, in_=i),
                         reads=[r_ps[bk]], writes=[r_rden[eb]] + ([r_hmT] if ti == 0 else []))
                    for dc in range(4):
                        ch = 4 * h + dc
                        bk = next_bank()
                        for mc in range(2):
                            P.op("pe", lambda e, o=ps[:, bk, :], l=vmm[:, mc, ch * 128:(ch + 1) * 128], r=expT[eb][:, mc, :],
                                 st=(mc == 0), sp_=(mc == 1): e.matmul(o, lhsT=l, rhs=r, start=st, stop=sp_),
                                 reads=[r_vmm, r_expT[eb]], writes=[r_ps[bk]])
                        P.op("dve", lambda e, o=hT[:, ch, :], i0=ps[:, bk, :], i1=rden[eb][:]:
                             e.tensor_tensor(out=o, in0=i0, in1=i1, op=ALU.mult),
                             reads=[r_ps[bk], r_rden[eb]], writes=[r_hT[ch]])
                proj("w_o", 0, KC, KC, 0, lambda k: hT[:, k, :], lambda k: r_hT[k], T, resid_add)

            if phases >= 3:
                rmsnorm(xT, r_xT, G_FFN, KC, T, hT, r_hT, 1.0 / D)
                for (f0, nf) in FF_GROUPS:
                    for pc in range(0, nf, PW // 128):
                        c0 = (f0 + pc) * 128
                        sg_ = next_panel("w_gate", 0, KC, c0)
                        for cc in range(PW // 128):
                            bg = next_bank()
                            for k in range(KC):
                                P.op("pe", lambda e, o=ps[:, bg, :], l=wsl[sg_][:, k, cc * 128:(cc + 1) * 128], r=hT[:, k, :],
                                     st=(k == 0), sp_=(k == KC - 1): e.matmul(o, lhsT=l, rhs=r, start=st, stop=sp_),
                                     reads=[r_wsl[sg_], r_hT[k]], writes=[r_ps[bg]])
                            P.op("act", lambda e, o=sgt[cc][:], i=ps[:, bg, :]: e.activation(out=o, in_=i, func=AF.Silu),
                                 reads=[r_ps[bg]], writes=[r_sgt[cc]])
                        su_ = next_panel("w_up", 0, KC, c0)
                        for cc in range(PW // 128):
                            j = pc + cc
                            bu = next_bank()
                            for k in range(KC):
                                P.op("pe", lambda e, o=ps[:, bu, :], l=wsl[su_][:, k, cc * 128:(cc + 1) * 128], r=hT[:, k, :],
                                     st=(k == 0), sp_=(k == KC - 1): e.matmul(o, lhsT=l, rhs=r, start=st, stop=sp_),
                                     reads=[r_wsl[su_], r_hT[k]], writes=[r_ps[bu]])
                            P.op("dve", lambda e, o=hid[:, j, :], i0=ps[:, bu, :], i1=sgt[cc][:]:
                                 e.tensor_tensor(out=o, in0=i0, in1=i1, op=ALU.mult),
                                 reads=[r_ps[bu], r_sgt[cc]], writes=[r_hid[j]])
                    proj("w_down", 0, KC, nf, f0 * 128, lambda k: hid[:, k, :], lambda k: r_hid[k], T, resid_add)

            for kc in range(KC):
                q = kc % 4
                P.op("act", lambda e, o=sq[q][:, :], i=xT[:, kc, :]: e.activation(out=o, in_=i, func=AF.Square),
                     reads=[r_xT[kc]], writes=[r_sq[q]])
                for blk in range(4):
                    P.op("pe", lambda e, o=ps[:, STAT, blk:blk + 1], l=sq[q][:, blk * 128:(blk + 1) * 128],
                         st=(kc == 0 and blk == 0), sp_=(kc == KC - 1):
                         e.matmul(o, lhsT=l, rhs=onesb[:, 0:1], start=st, stop=sp_, skip_group_check=True),
                         reads=[r_sq[q], r_ones], writes=[r_ps[STAT]])
            P.op("act", lambda e: e.activation(out=lnc[:], in_=ps[:, STAT, 0:4], func=AF.Ln, scale=1.0 / D,
                                               bias=eps_ap[:, 0:1]),
                 reads=[r_ps[STAT], r_eps], writes=[r_lnc])
            P.op("act", lambda e: e.activation(out=rstdc[:], in_=lnc[:], func=AF.Exp, scale=-0.5),
                 reads=[r_lnc], writes=[r_rstdc])
            for cq in range(4):
                for blk in range(4):
                    ob = (cq * 4 + blk) % 4
                    bk = next_bank()
                    for j in range(4):
                        kc = cq * 4 + j
                        P.op("pe", lambda e, o=ps[:, bk, j * 128:(j + 1) * 128], i=xT[:, kc, blk * 128:(blk + 1) * 128]:
                             e.transpose(out=o, in_=i, identity=identf[:]),
                             reads=[r_xT[kc], r_identf], writes=[r_ps[bk]])
                    P.op("dve", lambda e, o=xout[ob][:], i0=ps[:, bk, :],
                         s_=rstdc[:, blk:blk + 1], i1=gbc[:, cq * 512:(cq + 1) * 512]:
                         e.scalar_tensor_tensor(out=o, in0=i0, scalar=s_, in1=i1, op0=ALU.mult, op1=ALU.mult),
                         reads=[r_ps[bk], r_rstdc, r_gbc], writes=[r_xout[ob]])
                    P.op("sp", lambda e, i=xout[ob][:], o=y[t0 + blk * 128:t0 + (blk + 1) * 128,
                                                           cq * 512:(cq + 1) * 512]:
                         e.dma_start(out=o, in_=i), reads=[r_xout[ob]], dma=f"st{ob}")

        if record is not None:
            return None
        assert wpos["next"] == len(sched), (wpos, len(sched))
        P.op("sp", lambda e: None, writes=list(r_xout))

        P.resolve()
        sems = {e: es.enter_context(nc.semaphore(f"s_{e}")) for e in ("pe", "act", "dve")}
        dsem = {k: es.enter_context(nc.semaphore(f"d_{k}")) for k in P.dma_cnt}

        def body(engname):
            def run(e):
                rank = P.rank.get(engname, {})
                for idx, ins in enumerate(P.ins[engname]):
                    for (kind, key, val) in ins["waits"]:
                        if kind == "eng":
                            e.wait_ge(sems[key], P.rank[key][val])
                        else:
                            e.wait_ge(dsem[key], val)
                    r = ins["fn"](e)
                    if ins["dma"] is not None:
                        r.then_inc(dsem[ins["dma"]], 16)
                    elif idx in rank:
                        r.then_inc(sems[engname], 1)
            return run

        with nc.Block() as block:
            block.sync(body("sp"))
            block.gpsimd(body("pool"))
            block.tensor(body("pe"))
            block.scalar(body("act"))
            block.vector(body("dve"))
    return nc


_NC_CACHE = {}


def _get_nc():
    if "nc" not in _NC_CACHE:
        _NC_CACHE["nc"] = _build()
    return _NC_CACHE["nc"]


def kernel(x, mem, norm_mix_g, w_in, pool_w, pool_scale, sgu_norm_g, w_spatial, b_spatial, w_out,
           norm_xattn_g, norm_mem_g, w_q, w_k, w_v, w_o, norm_ffn_g, w_gate, w_up, w_down, final_norm_g):
    f = lambda a: np.ascontiguousarray(np.asarray(a, dtype=np.float32))
    x = f(x)
    mem = f(mem)
    B, S, _ = x.shape

    def fm(vec):
        v = f(vec).reshape(-1)
        return v.reshape(-1, 128).T

    gvh = np.ascontiguousarray(np.concatenate(
        [fm(norm_mix_g[0]), fm(norm_xattn_g[0]), fm(norm_mem_g[0]), fm(norm_ffn_g[0]), fm(final_norm_g),
         fm(pool_scale[0]), fm(sgu_norm_g[0])], axis=1).astype(np.float32))
    assert gvh.shape == (128, NG)
    wsT_h = np.ascontiguousarray(np.transpose(f(w_spatial)[0], (2, 0, 1)).reshape(128, 8 * 128))
    brow_h = f(b_spatial)[0].reshape(1, 1024)
    ident_h = np.eye(128, dtype=np.float32)
    shared = dict(
        w_in=f(w_in)[0], pool_w=f(pool_w)[0], w_out=f(w_out)[0], w_q=f(w_q)[0], w_k=f(w_k)[0], w_v=f(w_v)[0],
        w_o=f(w_o)[0], w_gate=f(w_gate)[0], w_up=f(w_up)[0], w_down=f(w_down)[0],
        gv=gvh, wsT=wsT_h, brow=brow_h, ident=ident_h,
        gbc=np.ascontiguousarray(np.broadcast_to(f(final_norm_g).reshape(1, D), (128, D))))
    in_maps = []
    for c in range(8):
        b, half = c // 2, c % 2
        s0 = half * TOK
        xc = x[b, s0:s0 + TOK]
        if half == 0:
            xhc = np.zeros((16, D), np.float32)
        else:
            xhc = x[b, s0 - 16:s0]
        pos = np.arange(s0 + 1, s0 + 17, dtype=np.float32)
        tab = np.concatenate([1.0 / np.minimum(pos, float(w)) for w in (2, 4, 8, 16)]).astype(np.float32)
        m = dict(shared)
        m.update(x=np.ascontiguousarray(xc), xh=np.ascontiguousarray(xhc), mem=np.ascontiguousarray(mem[b]),
                 invc=np.ascontiguousarray(np.broadcast_to(tab[None, :], (128, 64))))
        in_maps.append(m)
    nc = _get_nc()
    res = run_bass_kernel_spmd(nc, in_maps, core_ids=list(range(8)))
    out = np.empty((B, S, D), np.float32)
    for c in range(8):
        b, half = c // 2, c % 2
        out[b, half * TOK:(half + 1) * TOK] = res.results[c]["y"]
    return out
```
